# Optimizing a Trainium2 kernel written in Bass

```python
import math
import jax, jax.numpy as jnp
from jax import lax
import numpy as np

D_MODEL = 1024
BATCH = 4
SEQ = 8192
DEPTH = 2

CTX_LEN = 256
GRID_W = 64
HEAD_DIM = 64
LRU_WIDTH = 512
LRU_BLOCKS = 8
LRU_BLOCK = LRU_WIDTH // LRU_BLOCKS
CONV_W = 4
CONV_LEFT = 2
LRU_C = 8.0
GQA_HEADS = 8
GQA_KV_HEADS = 2
GQA_GROUP = GQA_HEADS // GQA_KV_HEADS
DIFF_HEADS = 4
DIFF_V_DIM = 2 * HEAD_DIM
BRANCH_W = 512
N_BRANCH = 3
D_FF = 4 * D_MODEL
Q_BLOCK = 128
ROPE_THETA = 10000.0
EPS = 1e-6
IN_SECTIONS = (LRU_WIDTH, LRU_WIDTH, GQA_HEADS * HEAD_DIM, GQA_KV_HEADS * HEAD_DIM, GQA_KV_HEADS * HEAD_DIM, DIFF_HEADS * 2 * HEAD_DIM, DIFF_HEADS * 2 * HEAD_DIM, DIFF_HEADS * DIFF_V_DIM, N_BRANCH * D_MODEL)
N_IN = 2 * LRU_WIDTH + (GQA_HEADS + 2 * GQA_KV_HEADS) * HEAD_DIM + DIFF_HEADS * (4 * HEAD_DIM + DIFF_V_DIM) + N_BRANCH * D_MODEL

kernel_name = 'hybrid_dit_rglru_gqa_diffattn'


def split_cols(z):
    idx, acc = [], 0
    for w in IN_SECTIONS[:-1]:
        acc += w
        idx.append(acc)
    return jnp.split(z, idx, axis=-1)


def rmsnorm(x, g):
    xf = x.astype(jnp.float32)
    y = xf * lax.rsqrt(jnp.mean(xf * xf, axis=-1, keepdims=True) + EPS)
    return (y * g.astype(jnp.float32)).astype(x.dtype)


def axial_rope_tables(rows):
    row = jnp.broadcast_to(jnp.arange(rows, dtype=jnp.float32)[:, None], (rows, GRID_W)).reshape(-1)
    col = jnp.broadcast_to(jnp.arange(GRID_W, dtype=jnp.float32)[None, :], (rows, GRID_W)).reshape(-1)
    n_freq = HEAD_DIM // 4
    inv = ROPE_THETA ** (-jnp.arange(n_freq, dtype=jnp.float32) * 2.0 / (HEAD_DIM // 2))
    ang = jnp.concatenate([row[:, None] * inv, col[:, None] * inv], axis=-1)
    return jnp.cos(ang), jnp.sin(ang)


def apply_axial_rope(x, cos, sin):
    n_freq = HEAD_DIM // 4
    xs = x.reshape(x.shape[:-1] + (2, 2, n_freq))
    x1 = xs[..., 0, :]
    x2 = xs[..., 1, :]
    cs = cos.reshape(cos.shape[0], 1, 2, n_freq).astype(x.dtype)
    sn = sin.reshape(sin.shape[0], 1, 2, n_freq).astype(x.dtype)
    out = jnp.stack([x1 * cs - x2 * sn, x1 * sn + x2 * cs], axis=-2)
    return out.reshape(x.shape)


def centred_dwconv(u, w, bias):
    t = u.shape[1]
    up = jnp.pad(u, ((0, 0), (CONV_LEFT, CONV_W - 1 - CONV_LEFT), (0, 0)))
    y = up[:, 0:t] * w[0]
    for j in range(1, CONV_W):
        y = y + up[:, j:j + t] * w[j]
    return y + bias


def block_diag_linear(u, w, bias):
    us = u.reshape(u.shape[:-1] + (LRU_BLOCKS, LRU_BLOCK))
    return jnp.einsum('btnc,ncd->btnd', us, w).reshape(u.shape) + bias


def rglru_coeffs(u, w_a, b_a, w_x, b_x, lam):
    uf = u.astype(jnp.float32)
    r = jax.nn.sigmoid(block_diag_linear(uf, w_a.astype(jnp.float32), b_a.astype(jnp.float32)))
    i = jax.nn.sigmoid(block_diag_linear(uf, w_x.astype(jnp.float32), b_x.astype(jnp.float32)))
    log_a = -LRU_C * r * jax.nn.softplus(-lam.astype(jnp.float32))
    a = jnp.exp(log_a)
    b = jnp.sqrt(-jnp.expm1(2.0 * log_a)) * (i * uf)
    return a, b


def _scan_combine(left, right):
    a_l, b_l = left
    a_r, b_r = right
    return a_l * a_r, a_r * b_l + b_r


def linear_scan(a, b, h0, reverse):
    if reverse:
        a = jnp.flip(a, axis=1)
        b = jnp.flip(b, axis=1)
    b = b.at[:, 0].add(a[:, 0] * h0)
    _, h = lax.associative_scan(_scan_combine, (a, b), axis=1)
    if reverse:
        h = jnp.flip(h, axis=1)
    return h


def gqa_attend(q, k, v):
    s = jnp.einsum('bqgrd,bkgd->bgrqk', q, k).astype(jnp.float32) * (HEAD_DIM ** -0.5)
    p = jax.nn.softmax(s, axis=-1).astype(v.dtype)
    return jnp.einsum('bgrqk,bkgd->bqgrd', p, v)


def diff_attend(q1, q2, k1, k2, v, lam):
    scale = HEAD_DIM ** -0.5
    p1 = jax.nn.softmax(jnp.einsum('bqhd,bkhd->bhqk', q1, k1).astype(jnp.float32) * scale, axis=-1)
    p2 = jax.nn.softmax(jnp.einsum('bqhd,bkhd->bhqk', q2, k2).astype(jnp.float32) * scale, axis=-1)
    p = (p1 - lam * p2).astype(v.dtype)
    return jnp.einsum('bhqk,bkhd->bqhd', p, v)


def blocked_queries(fn, qs):
    bsz, t = qs[0].shape[:2]
    nb = t // Q_BLOCK
    blocks = tuple(jnp.moveaxis(a.reshape((bsz, nb, Q_BLOCK) + a.shape[2:]), 1, 0) for a in qs)
    out = lax.map(lambda blk: fn(*blk), blocks)
    out = jnp.moveaxis(out, 0, 1)
    return out.reshape((bsz, t) + out.shape[3:])


def sq_relu_mlp(h, w_up, w_down):
    return jnp.square(jax.nn.relu(h @ w_up)) @ w_down


def token_mixer(hl, hc, p, lam_init, cos, sin, ctx_out):
    bsz, s, _ = hl.shape
    n_ctx = hc.shape[1]
    zl = split_cols(hl @ p['w_in'])
    zc = split_cols(hc @ p['w_in'])

    def heads(z, n):
        return z.reshape(z.shape[:2] + (n, HEAD_DIM))

    ul = centred_dwconv(zl[0], p['conv_w'], p['conv_b'])
    uc = centred_dwconv(zc[0], p['conv_w'], p['conv_b'])
    hs_l, hs_c = [], []
    for d, rev in ((0, False), (1, True)):
        gp = (p['w_rg'][d], p['b_rg'][d], p['w_ig'][d], p['b_ig'][d], p['lru_lambda'][d])
        a_c, b_c = rglru_coeffs(uc, *gp)
        h_c = linear_scan(a_c, b_c, jnp.zeros((bsz, LRU_WIDTH), jnp.float32), rev)
        h0 = h_c[:, 0] if rev else h_c[:, -1]
        a_l, b_l = rglru_coeffs(ul, *gp)
        hs_l.append(linear_scan(a_l, b_l, h0, rev))
        hs_c.append(h_c)
    y_rec_l = ((hs_l[0] + hs_l[1]) * jax.nn.gelu(zl[1].astype(jnp.float32))).astype(hl.dtype)

    ql = apply_axial_rope(rmsnorm(heads(zl[2], GQA_HEADS), p['q_norm_g']), cos, sin)
    kl = apply_axial_rope(rmsnorm(heads(zl[3], GQA_KV_HEADS), p['k_norm_g']), cos, sin)
    vl = heads(zl[4], GQA_KV_HEADS)
    qc = rmsnorm(heads(zc[2], GQA_HEADS), p['q_norm_g'])
    kc = rmsnorm(heads(zc[3], GQA_KV_HEADS), p['k_norm_g'])
    vc = heads(zc[4], GQA_KV_HEADS)
    k_all = jnp.concatenate([kc, kl], axis=1)
    v_all = jnp.concatenate([vc, vl], axis=1)

    def gqa_block(qb):
        qg = qb.reshape(qb.shape[:2] + (GQA_KV_HEADS, GQA_GROUP, HEAD_DIM))
        return gqa_attend(qg, k_all, v_all)

    y_gqa_l = blocked_queries(gqa_block, (ql,)).reshape(bsz, s, GQA_HEADS * HEAD_DIM)

    lam = (jnp.exp(jnp.sum(p['lambda_q1'].astype(jnp.float32) * p['lambda_k1'].astype(jnp.float32)))
           - jnp.exp(jnp.sum(p['lambda_q2'].astype(jnp.float32) * p['lambda_k2'].astype(jnp.float32)))
           + lam_init)
    dql = apply_axial_rope(heads(zl[5], 2 * DIFF_HEADS), cos, sin).reshape(bsz, s, DIFF_HEADS, 2, HEAD_DIM)
    dkl = apply_axial_rope(heads(zl[6], 2 * DIFF_HEADS), cos, sin).reshape(bsz, s, DIFF_HEADS, 2, HEAD_DIM)
    dvl = zl[7].reshape(bsz, s, DIFF_HEADS, DIFF_V_DIM)
    dqc = zc[5].reshape(bsz, n_ctx, DIFF_HEADS, 2, HEAD_DIM)
    dkc = zc[6].reshape(bsz, n_ctx, DIFF_HEADS, 2, HEAD_DIM)
    dvc = zc[7].reshape(bsz, n_ctx, DIFF_HEADS, DIFF_V_DIM)
    dk_all = jnp.concatenate([dkc, dkl], axis=1)
    k1_all = dk_all[:, :, :, 0]
    k2_all = dk_all[:, :, :, 1]
    dv_all = jnp.concatenate([dvc, dvl], axis=1)

    def diff_block(q1b, q2b):
        return diff_attend(q1b, q2b, k1_all, k2_all, dv_all, lam)

    o_l = blocked_queries(diff_block, (dql[:, :, :, 0], dql[:, :, :, 1]))
    y_diff_l = (rmsnorm(o_l, p['subln_g']) * (1.0 - lam_init)).reshape(bsz, s, DIFF_HEADS * DIFF_V_DIM)

    def merge(ys, zg):
        g = jax.nn.sigmoid(zg + p['b_gate']).reshape(zg.shape[:2] + (N_BRANCH, D_MODEL))
        m = g[:, :, 0] * (ys[0] @ p['w_branch'][0])
        for n in range(1, N_BRANCH):
            m = m + g[:, :, n] * (ys[n] @ p['w_branch'][n])
        return m @ p['w_out']

    out_l = merge((y_rec_l, y_gqa_l, y_diff_l), zl[8])
    if not ctx_out:
        return out_l, None

    y_rec_c = ((hs_c[0] + hs_c[1]) * jax.nn.gelu(zc[1].astype(jnp.float32))).astype(hc.dtype)
    qcg = qc.reshape(bsz, n_ctx, GQA_KV_HEADS, GQA_GROUP, HEAD_DIM)
    y_gqa_c = gqa_attend(qcg, kc, vc).reshape(bsz, n_ctx, GQA_HEADS * HEAD_DIM)
    o_c = diff_attend(dqc[:, :, :, 0], dqc[:, :, :, 1], dkc[:, :, :, 0], dkc[:, :, :, 1], dvc, lam)
    y_diff_c = (rmsnorm(o_c, p['subln_g']) * (1.0 - lam_init)).reshape(bsz, n_ctx, DIFF_HEADS * DIFF_V_DIM)
    out_c = merge((y_rec_c, y_gqa_c, y_diff_c), zc[8])
    return out_l, out_c


def setup_inputs(seed: int = 0) -> dict:
    key = jax.random.key(seed)
    ks = jax.random.split(key, 32)
    f32 = jnp.float32

    def nrm(k, shape, scale):
        return jax.random.normal(k, shape, f32) * scale

    u = jax.random.uniform(ks[15], (DEPTH, 2, LRU_WIDTH), f32, 0.9, 0.999)
    a = u ** (1.0 / LRU_C)
    lru_lambda = jnp.log(a) - jnp.log1p(-a)
    return {
        'x': nrm(ks[0], (BATCH, SEQ, D_MODEL), 1.0),
        'c': nrm(ks[1], (BATCH, D_MODEL), 1.0),
        'ctx': nrm(ks[2], (BATCH, CTX_LEN, D_MODEL), 1.0),
        'c_ctx': nrm(ks[3], (D_MODEL,), 1.0),
        'w_mod': nrm(ks[4], (DEPTH, D_MODEL, 6 * D_MODEL), 0.5 * D_MODEL ** -0.5),
        'b_mod': nrm(ks[5], (DEPTH, 6 * D_MODEL), 0.02),
        'norm1_g': 1.0 + nrm(ks[6], (DEPTH, D_MODEL), 0.1),
        'w_in': nrm(ks[7], (DEPTH, D_MODEL, N_IN), D_MODEL ** -0.5),
        'b_gate': nrm(ks[8], (DEPTH, N_BRANCH * D_MODEL), 0.1),
        'conv_w': nrm(ks[9], (DEPTH, CONV_W, LRU_WIDTH), CONV_W ** -0.5),
        'conv_b': nrm(ks[10], (DEPTH, LRU_WIDTH), 0.02),
        'w_rg': nrm(ks[11], (DEPTH, 2, LRU_BLOCKS, LRU_BLOCK, LRU_BLOCK), LRU_BLOCK ** -0.5),
        'b_rg': nrm(ks[12], (DEPTH, 2, LRU_WIDTH), 0.1),
        'w_ig': nrm(ks[13], (DEPTH, 2, LRU_BLOCKS, LRU_BLOCK, LRU_BLOCK), LRU_BLOCK ** -0.5),
        'b_ig': nrm(ks[14], (DEPTH, 2, LRU_WIDTH), 0.1),
        'lru_lambda': lru_lambda,
        'q_norm_g': 1.0 + nrm(ks[16], (DEPTH, HEAD_DIM), 0.1),
        'k_norm_g': 1.0 + nrm(ks[17], (DEPTH, HEAD_DIM), 0.1),
        'lambda_q1': nrm(ks[18], (DEPTH, HEAD_DIM), 0.1),
        'lambda_k1': nrm(ks[19], (DEPTH, HEAD_DIM), 0.1),
        'lambda_q2': nrm(ks[20], (DEPTH, HEAD_DIM), 0.1),
        'lambda_k2': nrm(ks[21], (DEPTH, HEAD_DIM), 0.1),
        'subln_g': 1.0 + nrm(ks[22], (DEPTH, DIFF_V_DIM), 0.1),
        'w_branch': nrm(ks[23], (DEPTH, N_BRANCH, BRANCH_W, D_MODEL), BRANCH_W ** -0.5),
        'w_out': nrm(ks[24], (DEPTH, D_MODEL, D_MODEL), D_MODEL ** -0.5),
        'norm2_g': 1.0 + nrm(ks[25], (DEPTH, D_MODEL), 0.1),
        'w_up': nrm(ks[26], (DEPTH, D_MODEL, D_FF), D_MODEL ** -0.5),
        'w_down': nrm(ks[27], (DEPTH, D_FF, D_MODEL), D_FF ** -0.5),
        'final_g': 1.0 + nrm(ks[28], (D_MODEL,), 0.1),
    }


def reference(x, c, ctx, c_ctx, w_mod, b_mod, norm1_g, w_in, b_gate, conv_w, conv_b, w_rg, b_rg, w_ig, b_ig, lru_lambda, q_norm_g, k_norm_g, lambda_q1, lambda_k1, lambda_q2, lambda_k2, subln_g, w_branch, w_out, norm2_g, w_up, w_down, final_g):
    rows = x.shape[1] // GRID_W
    cos, sin = axial_rope_tables(rows)
    sc = jax.nn.silu(c)
    scc = jax.nn.silu(c_ctx)
    xc = ctx
    for l in range(DEPTH):
        last = l == DEPTH - 1
        lam_init = 0.8 - 0.6 * math.exp(-0.3 * l)
        mod_l = [m[:, None, :] for m in jnp.split(sc @ w_mod[l] + b_mod[l], 6, axis=-1)]
        mod_c = [m[None, None, :] for m in jnp.split(scc @ w_mod[l] + b_mod[l], 6, axis=-1)]
        p = {
            'w_in': w_in[l], 'b_gate': b_gate[l], 'conv_w': conv_w[l], 'conv_b': conv_b[l],
            'w_rg': w_rg[l], 'b_rg': b_rg[l], 'w_ig': w_ig[l], 'b_ig': b_ig[l],
            'lru_lambda': lru_lambda[l], 'q_norm_g': q_norm_g[l], 'k_norm_g': k_norm_g[l],
            'lambda_q1': lambda_q1[l], 'lambda_k1': lambda_k1[l],
            'lambda_q2': lambda_q2[l], 'lambda_k2': lambda_k2[l],
            'subln_g': subln_g[l], 'w_branch': w_branch[l], 'w_out': w_out[l],
        }
        hl = rmsnorm(x, norm1_g[l]) * (1.0 + mod_l[1]) + mod_l[0]
        hc = rmsnorm(xc, norm1_g[l]) * (1.0 + mod_c[1]) + mod_c[0]
        yl, yc = token_mixer(hl, hc, p, lam_init, cos, sin, not last)
        x = x + mod_l[2] * yl
        x = x + mod_l[5] * sq_relu_mlp(rmsnorm(x, norm2_g[l]) * (1.0 + mod_l[4]) + mod_l[3], w_up[l], w_down[l])
        if not last:
            xc = xc + mod_c[2] * yc
            xc = xc + mod_c[5] * sq_relu_mlp(rmsnorm(xc, norm2_g[l]) * (1.0 + mod_c[4]) + mod_c[3], w_up[l], w_down[l])
    return rmsnorm(x, final_g)
```

```python
from contextlib import ExitStack
import math
import numpy as np
import concourse.bass as bass
import concourse.mybir as mybir
from concourse.bass_utils import run_bass_kernel_spmd

F32 = mybir.dt.float32
BF16 = mybir.dt.bfloat16
AF = mybir.ActivationFunctionType
ALU = mybir.AluOpType
AX = mybir.AxisListType

D_MODEL = 1024
DEPTH = 2
CTX_LEN = 256
GRID_W = 64
EPS = 1e-6
NA = 3456
OFF_GQ, OFF_GK, OFF_GV, OFF_DQ, OFF_DK, OFF_DV, OFF_LX, OFF_LG = 0, 512, 768, 896, 1408, 1920, 2432, 2944

ENGS = ("pe", "act", "dve", "pool", "sp")
EIDX = {e: i for i, e in enumerate(ENGS)}
NE = len(ENGS)
NDMA = 40


class Buf:
    __slots__ = ("name", "last_w", "readers")

    def __init__(self, name=""):
        self.name = name
        self.last_w = None
        self.readers = []


def bufs(n):
    return [Buf() for _ in range(n)]


class _Op:
    __slots__ = ("eng", "fn", "deps", "is_dma", "eidx", "dslot", "dval", "waits", "clock", "awaited")


class _Recorder:
    def __getattr__(self, name):
        return lambda *a, **k: (name, a, k)


_REC = _Recorder()


class Prog:
    def __init__(self, nc):
        self.nc = nc
        self._stack = []
        self.sems = []
        for e in ENGS:
            cm = nc.semaphore("s_" + e)
            self.sems.append(cm.__enter__())
            self._stack.append(cm)
        self.dsems = []
        for k in range(NDMA):
            cm = nc.semaphore("d_%d" % k)
            self.dsems.append(cm.__enter__())
            self._stack.append(cm)
        self.ops = []
        self.n_eng = [0] * NE
        self.sig_cnt = [0] * NE
        self.sigmap = [dict() for _ in range(NE)]
        self.last_sig = [0] * NE
        self.n_dma = 0
        self.dma_val = [0] * NDMA
        self.dma_last_op = [None] * NDMA
        self.known = [[0] * (NE + NDMA) for _ in range(NE)]
        self.first_flush = True
        self.base = 0
        self.n_instr = 0

    def close(self):
        for cm in reversed(self._stack):
            cm.__exit__(None, None, None)

    def _rec(self, eng, fn, reads, writes, is_dma):
        o = _Op()
        o.eng = EIDX[eng]
        o.fn = fn(_REC)
        o.is_dma = is_dma
        deps = set()
        oid = self.base + len(self.ops)
        for b in reads:
            if b.last_w is not None:
                deps.add(b.last_w)
        for b in writes:
            if b.last_w is not None:
                deps.add(b.last_w)
            deps.update(b.readers)
        for b in reads:
            b.readers.append(oid)
        for b in writes:
            b.last_w = oid
            b.readers = []
        deps.discard(oid)
        o.deps = deps
        o.awaited = False
        self.ops.append(o)
        return oid

    def op(self, eng, fn, reads=(), writes=()):
        return self._rec(eng, fn, reads, writes, False)

    def dma(self, eng, fn, reads=(), writes=()):
        return self._rec(eng, fn, reads, writes, True)

    def flush(self, final=False):
        ops = self.ops
        known = self.known
        bar = [0] * (NE + NDMA)
        for e in range(NE):
            bar[e] = self.n_eng[e]
        for k in range(NDMA):
            bar[NE + k] = self.dma_val[k]
        bar_waits = [[] for _ in range(NE)]
        if not self.first_flush:
            for e in range(NE):
                for e2 in range(NE):
                    if known[e][e2] < bar[e2]:
                        bar_waits[e].append(("e", e2, self.last_sig[e2]))
                for k in range(NDMA):
                    if known[e][NE + k] < bar[NE + k]:
                        bar_waits[e].append(("d", k, bar[NE + k]))
                known[e] = list(bar)
        self.first_flush = False
        base = self.base
        NV = NE + NDMA
        for oi, o in enumerate(ops):
            oid = base + oi
            E = o.eng
            kn = known[E]
            need_e = {}
            need_d = {}
            deps = o.deps
            if o.is_dma:
                k = self.n_dma % NDMA
                self.n_dma += 1
                o.dslot = k
                o.dval = self.dma_val[k] + 16
                prev = self.dma_last_op[k]
                if prev is not None and prev >= base:
                    deps = set(deps)
                    deps.add(prev)
                self.dma_val[k] = o.dval
                self.dma_last_op[k] = oid
            for d in deps:
                if d < base:
                    continue
                od = ops[d - base]
                if od.is_dma:
                    if kn[NE + od.dslot] < od.dval:
                        if need_d.get(od.dslot, (0, None))[0] < od.dval:
                            need_d[od.dslot] = (od.dval, d)
                else:
                    E2 = od.eng
                    if E2 == E and E == 0 and not o.is_dma:
                        continue
                    if kn[E2] < od.eidx + 1:
                        if need_e.get(E2, (0, None))[0] < od.eidx + 1:
                            need_e[E2] = (od.eidx + 1, d)
            waits = []
            for E2, (v, d) in need_e.items():
                ops[d - base].awaited = True
                waits.append(("e", E2, ops[d - base].eidx, d))
            for k, (v, d) in need_d.items():
                waits.append(("d", k, v, d))
            for w in waits:
                ck = ops[w[3] - base].clock
                for i in range(NV):
                    if ck[i] > kn[i]:
                        kn[i] = ck[i]
            o.waits = waits
            ck = list(kn)
            if o.is_dma:
                ck[NE + o.dslot] = o.dval
                o.eidx = -1
            else:
                o.eidx = self.n_eng[E]
                self.n_eng[E] += 1
                ck[E] = o.eidx + 1
            o.clock = ck
        last = [None] * NE
        for o in ops:
            if not o.is_dma:
                last[o.eng] = o
        for o in last:
            if o is not None:
                o.awaited = True
        for o in ops:
            if not o.is_dma and o.awaited:
                self.sig_cnt[o.eng] += 1
                self.sigmap[o.eng][o.eidx] = self.sig_cnt[o.eng]
        for e in range(NE):
            self.last_sig[e] = self.sig_cnt[e]
        self._emit(ops, bar_waits, final)
        self.base += len(ops)
        self.ops = []

    def _emit(self, ops, bar_waits, final):
        nc = self.nc
        per = [[] for _ in range(NE)]
        for o in ops:
            per[o.eng].append(o)
        sems = self.sems
        dsems = self.dsems
        sigmap = self.sigmap
        cnt = [0]

        def run(E, eng):
            for w in bar_waits[E]:
                if w[0] == "e":
                    if w[2] > 0:
                        eng.wait_ge(sems[w[1]], w[2])
                        cnt[0] += 1
                else:
                    eng.wait_ge(dsems[w[1]], w[2])
                    cnt[0] += 1
            for o in per[E]:
                for w in o.waits:
                    if w[0] == "e":
                        eng.wait_ge(sems[w[1]], sigmap[w[1]][w[2]])
                    else:
                        eng.wait_ge(dsems[w[1]], w[2])
                    cnt[0] += 1
                ins = getattr(eng, o.fn[0])(*o.fn[1], **o.fn[2])
                o.fn = None
                cnt[0] += 1
                if o.is_dma:
                    ins.then_inc(dsems[o.dslot], 16)
                elif o.awaited:
                    ins.then_inc(sems[E], 1)
            if final:
                for k in range(NDMA):
                    if self.dma_val[k] > 0:
                        eng.wait_ge(dsems[k], self.dma_val[k])

        with nc.Block() as block:
            @block.tensor
            def _(e):
                run(0, e)

            @block.scalar
            def _(e):
                run(1, e)

            @block.vector
            def _(e):
                run(2, e)

            @block.gpsimd
            def _(e):
                run(3, e)

            @block.sync
            def _(e):
                run(4, e)
        self.n_instr += cnt[0]


_UID = [0]


def _uname(name):
    _UID[0] += 1
    return "%s_%d" % (name, _UID[0])


class Rot:
    def __init__(self, es, nc, name, shape, dtype, n, psum=False):
        self.tiles = []
        self.bufs = []
        for i in range(n):
            mk = nc.psum_tensor if psum else nc.sbuf_tensor
            self.tiles.append(es.enter_context(mk(_uname("%s%d" % (name, i)), shape, dtype)))
            self.bufs.append(Buf())
        self.i = 0

    def next(self):
        k = self.i % len(self.tiles)
        self.i += 1
        return self.tiles[k], self.bufs[k]


def sb(es, nc, name, shape, dtype):
    return es.enter_context(nc.sbuf_tensor(_uname(name), shape, dtype))


def pstile(es, nc, name, shape, dtype):
    return es.enter_context(nc.psum_tensor(_uname(name), shape, dtype))


def token_tiles(S, C, width, with_ctx=True):
    tl = []
    if with_ctx:
        for t0 in range(0, C, width):
            tl.append((t0, min(width, C - t0)))
    for t0 in range(C, C + S, width):
        tl.append((t0, min(width, C + S - t0)))
    return tl


def build_program(S, C, L):
    T = C + S
    TT = T // 128
    nc = bass.Bass("TRN2", target_bir_lowering=False)

    def din(name, shape, dt=F32):
        return nc.dram_tensor(name, list(shape), dt, kind="ExternalInput")

    xin = din("xin", [D_MODEL, T])
    cvec = din("cvec", [128, 8, 2])
    w_mod = din("w_mod", [L, D_MODEL, 6144])
    b_mod = din("b_mod", [L, 128, 48])
    g1 = din("g1", [L, 128, 8])
    g2 = din("g2", [L, 128, 8])
    gfin = din("gfin", [128, 8])
    w_a = din("w_a", [L, D_MODEL, NA])
    w_g = din("w_g", [L, D_MODEL, 3072])
    b_gate = din("b_gate", [L, 128, 24])
    convp = din("convp", [L, 128, 4, 5])
    w_rgbd = din("w_rgbd", [L, 2, 4, 128, 128])
    w_igbd = din("w_igbd", [L, 2, 4, 128, 128])
    lruv = din("lruv", [L, 128, 2, 4, 3])
    qkg = din("qkg", [L, 2, 64])
    lamv = din("lamv", [L, 4, 64])
    subg = din("subg", [L, 128, 1])
    w_br = din("w_br", [L, 3, 512, D_MODEL])
    w_out = din("w_out", [L, D_MODEL, D_MODEL])
    w_up = din("w_up", [L, D_MODEL, 4096])
    w_down = din("w_down", [L, 4096, D_MODEL])
    rope = din("rope", [T, 64])
    outT = nc.dram_tensor("outT", [D_MODEL, S], F32, kind="ExternalOutput")

    def scr(name, shape, dt):
        return nc.dram_tensor(name, list(shape), dt)

    x1s = scr("x1s", [D_MODEL, T], F32)
    x2s = scr("x2s", [D_MODEL, T], F32)
    lxs = scr("lxs", [512, T], F32)
    lgs = scr("lgs", [512, T], F32)
    qgs = scr("qgs", [4, 128, T], BF16)
    kgs = scr("kgs", [2, 128, T], BF16)
    vgs = scr("vgs", [T, 256], BF16)
    dqs = scr("dqs", [4, 128, T], BF16)
    dks = scr("dks", [4, 128, T], BF16)
    dvs = scr("dvs", [T, 512], BF16)
    yrs = scr("yrs", [512, T], BF16)
    yas = scr("yas", [512, T], BF16)
    yds = scr("yds", [512, T], BF16)

    P = Prog(nc)
    top = ExitStack()

    ones_bf = sb(top, nc, "ones_bf", [128, 128], BF16)
    ones_f = sb(top, nc, "ones_f", [128, 128], F32)
    ident_f = sb(top, nc, "ident_f", [128, 128], F32)
    ident_bf = sb(top, nc, "ident_bf", [128, 128], BF16)
    scv = sb(top, nc, "scv", [128, 8, 2], F32)
    modv = sb(top, nc, "modv", [128, 48, 2], F32)
    G1 = sb(top, nc, "G1", [128, 8, 2], F32)
    G2 = sb(top, nc, "G2", [128, 8, 2], F32)
    GF = sb(top, nc, "GF", [128, 8], F32)
    g1t = sb(top, nc, "g1t", [128, 8], F32)
    g2t = sb(top, nc, "g2t", [128, 8], F32)
    bmt = sb(top, nc, "bmt", [128, 48], F32)
    bgt = sb(top, nc, "bgt", [128, 24], F32)
    cvt = sb(top, nc, "cvt", [128, 4, 5], F32)
    lvt = sb(top, nc, "lvt", [128, 2, 4, 3], F32)
    cch = sb(top, nc, "cch", [128, 2, 4], F32)
    qkgt = sb(top, nc, "qkgt", [128, 2, 64], F32)
    lamt = sb(top, nc, "lamt", [1, 4, 64], F32)
    lamw = sb(top, nc, "lamw", [1, 8], F32)
    neglam = sb(top, nc, "neglam", [128, 1], F32)
    gsub = sb(top, nc, "gsub", [128, 1], F32)
    b_const = Buf()
    b_vec = Buf()

    P.op("pool", lambda e: e.memset(ones_bf[:], 1.0), writes=[b_const])
    P.op("pool", lambda e: e.memset(ones_f[:], 1.0), writes=[b_const])
    P.op("pool", lambda e: e.memset(ident_f[:], 1.0), writes=[b_const])
    P.op("pool", lambda e: e.affine_select(out=ident_f[:], in_=ident_f[:], pattern=[[-1, 128]],
                                           compare_op=ALU.is_equal, fill=0.0, base=0, channel_multiplier=1),
         reads=[b_const], writes=[b_const])
    P.op("dve", lambda e: e.tensor_copy(out=ident_bf[:], in_=ident_f[:]), reads=[b_const], writes=[b_const])
    P.dma("sp", lambda e: e.dma_start(out=scv[:], in_=cvec.ap()), writes=[b_vec])
    P.dma("sp", lambda e: e.dma_start(out=GF[:], in_=gfin.ap()), writes=[b_vec])
    P.op("act", lambda e: e.activation(out=scv[:], in_=scv[:], func=AF.Silu), reads=[b_vec], writes=[b_vec])
    P.op("dve", lambda e: e.tensor_scalar(out=GF[:], in0=GF[:], scalar1=32.0, scalar2=None, op0=ALU.mult),
         reads=[b_vec], writes=[b_vec])
    P.flush()

    def norm_mod(xt, bx, n, Gt, SHt, col, hT, bh, sq_rot, ps_rot, rs_rot, tmp_rot):
        sq, bsq = sq_rot.next()
        half = 4
        for hh in range(2):
            P.op("act", lambda e, hh=hh: e.activation(out=sq[:, hh * half:(hh + 1) * half, :n],
                                                      in_=xt[:, hh * half:(hh + 1) * half, :n], func=AF.Square),
                 reads=bx, writes=[bsq[hh]])
        ps, bps = ps_rot.next()
        for c in range(8):
            P.op("pe", lambda e, c=c: e.matmul(ps[:, :n], lhsT=ones_bf[:], rhs=sq[:, c, :n], start=(c == 0), stop=(c == 7)),
                 reads=[bsq[c // half]], writes=[bps])
        rs, brs = rs_rot.next()
        P.op("act", lambda e: e.activation(out=rs[:, :n], in_=ps[:, :n], func=AF.Sqrt, scale=1.0, bias=1024.0 * EPS),
             reads=[bps], writes=[brs])
        P.op("dve", lambda e: e.reciprocal(out=rs[:, :n], in_=rs[:, :n]), reads=[brs], writes=[brs])
        for c in range(8):
            tmp, btmp = tmp_rot.next()
            P.op("dve", lambda e, c=c, tmp=tmp: e.scalar_tensor_tensor(out=tmp[:, :n], in0=xt[:, c, :n], scalar=Gt[:, c, col:col + 1],
                                                                    in1=rs[:, :n], op0=ALU.mult, op1=ALU.mult),
                 reads=bx + [brs, b_vec], writes=[btmp])
            P.op("act", lambda e, c=c, tmp=tmp: e.activation(out=hT[:, c, :n], in_=tmp[:, :n], func=AF.Identity,
                                                             bias=SHt[:, c, col:col + 1], scale=1.0),
                 reads=[btmp, b_vec], writes=[bh[c]])

    def load_w_bf16(wt, src_l, ncols, bw, nchunk=8):
        for c in range(nchunk):
            P.dma("pool", lambda e, c=c: e.dma_start(out=wt[:, c, :], in_=src_l[c * 128:(c + 1) * 128, :]), writes=[bw])

    for l in range(L):
        last = (l == L - 1)
        lam_init = 0.8 - 0.6 * math.exp(-0.3 * l)
        Xl = xin if l == 0 else x2s

        with ExitStack() as es:
            wm_rot = Rot(es, nc, "wm", [128, 8, 1024], F32, 2)
            pm = pstile(es, nc, "pm", [128, 16], F32)
            bpm = Buf()
            P.dma("sp", lambda e: e.dma_start(out=bmt[:], in_=b_mod.ap()[l]), writes=[b_vec])
            P.dma("sp", lambda e: e.dma_start(out=g1t[:], in_=g1.ap()[l]), writes=[b_vec])
            P.dma("sp", lambda e: e.dma_start(out=g2t[:], in_=g2.ap()[l]), writes=[b_vec])
            P.dma("sp", lambda e: e.dma_start(out=bgt[:], in_=b_gate.ap()[l]), writes=[b_vec])
            P.dma("sp", lambda e: e.dma_start(out=cvt[:], in_=convp.ap()[l]), writes=[b_vec])
            P.dma("sp", lambda e: e.dma_start(out=lvt[:], in_=lruv.ap()[l]), writes=[b_vec])
            P.dma("sp", lambda e: e.dma_start(out=gsub[:], in_=subg.ap()[l]), writes=[b_vec])
            P.dma("sp", lambda e: e.dma_start(out=lamt[:], in_=bass.AP(lamv, l * 256, [[256, 1], [64, 4], [1, 64]])), writes=[b_vec])
            P.dma("sp", lambda e: e.dma_start(out=qkgt[:], in_=bass.AP(qkg, l * 128, [[0, 128], [64, 2], [1, 64]])), writes=[b_vec])
            for grp in range(6):
                wm, bwm = wm_rot.next()
                P.dma("sp", lambda e, wm=wm, grp=grp: e.dma_start(
                    out=wm[:], in_=w_mod.ap()[l][:, grp * 1024:(grp + 1) * 1024].rearrange("(c p) n -> p c n", p=128)), writes=[bwm])
                for j in range(8):
                    for c in range(8):
                        P.op("pe", lambda e, wm=wm, j=j, c=c: e.matmul(pm[:, 2 * j:2 * j + 2], lhsT=wm[:, c, j * 128:(j + 1) * 128],
                                                                      rhs=scv[:, c, :], start=(c == 0), stop=(c == 7)),
                             reads=[bwm, b_vec], writes=[bpm])
                P.op("dve", lambda e, grp=grp: e.tensor_tensor(out=modv[:, grp * 8:(grp + 1) * 8, :], in0=pm[:].rearrange("p (j t) -> p j t", t=2),
                                                              in1=bass.AP(bmt, grp * 8, [[48, 128], [1, 8], [0, 2]]), op=ALU.add),
                     reads=[bpm, b_vec], writes=[b_vec])
            for (Gx, gx, sec) in ((G1, g1t, 1), (G2, g2t, 4)):
                P.op("dve", lambda e, Gx=Gx, sec=sec: e.tensor_scalar(out=Gx[:], in0=modv[:, sec * 8:(sec + 1) * 8, :], scalar1=1.0, scalar2=32.0,
                                                                      op0=ALU.add, op1=ALU.mult), reads=[b_vec], writes=[b_vec])
                P.op("dve", lambda e, Gx=Gx, gx=gx: e.tensor_tensor(out=Gx[:], in0=Gx[:], in1=bass.AP(gx, 0, [[8, 128], [1, 8], [0, 2]]), op=ALU.mult),
                     reads=[b_vec], writes=[b_vec])
            P.op("act", lambda e: e.activation(out=cch[:], in_=lvt[:, :, :, 2], func=AF.Exp, scale=-1.0), reads=[b_vec], writes=[b_vec])
            P.op("act", lambda e: e.activation(out=cch[:], in_=cch[:], func=AF.Ln, bias=1.0, scale=1.0), reads=[b_vec], writes=[b_vec])
            P.op("dve", lambda e: e.tensor_scalar(out=cch[:], in0=cch[:], scalar1=-8.0, scalar2=None, op0=ALU.mult), reads=[b_vec], writes=[b_vec])
            P.op("dve", lambda e: e.tensor_tensor(out=lamt[:, 0, :], in0=lamt[:, 0, :], in1=lamt[:, 1, :], op=ALU.mult), reads=[b_vec], writes=[b_vec])
            P.op("dve", lambda e: e.tensor_tensor(out=lamt[:, 2, :], in0=lamt[:, 2, :], in1=lamt[:, 3, :], op=ALU.mult), reads=[b_vec], writes=[b_vec])
            P.op("dve", lambda e: e.tensor_reduce(out=lamw[:, 0:1], in_=lamt[:, 0, :], axis=AX.X, op=ALU.add), reads=[b_vec], writes=[b_vec])
            P.op("dve", lambda e: e.tensor_reduce(out=lamw[:, 1:2], in_=lamt[:, 2, :], axis=AX.X, op=ALU.add), reads=[b_vec], writes=[b_vec])
            P.op("act", lambda e: e.activation(out=lamw[:, 2:4], in_=lamw[:, 0:2], func=AF.Exp), reads=[b_vec], writes=[b_vec])
            P.op("dve", lambda e: e.tensor_tensor(out=lamw[:, 4:5], in0=lamw[:, 3:4], in1=lamw[:, 2:3], op=ALU.subtract), reads=[b_vec], writes=[b_vec])
            P.op("dve", lambda e: e.tensor_scalar(out=lamw[:, 5:6], in0=lamw[:, 4:5], scalar1=-lam_init, scalar2=None, op0=ALU.add), reads=[b_vec], writes=[b_vec])
            P.op("pe", lambda e: e.matmul(pm[:, 0:1], lhsT=ones_f[0:1, :], rhs=lamw[0:1, 5:6], start=True, stop=True), reads=[b_vec, b_const], writes=[bpm])
            P.op("dve", lambda e: e.tensor_copy(out=neglam[:], in_=pm[:, 0:1]), reads=[bpm], writes=[b_vec])
            P.op("dve", lambda e: e.tensor_scalar(out=gsub[:], in0=gsub[:], scalar1=math.sqrt(128.0) * (1.0 - lam_init), scalar2=None, op0=ALU.mult),
                 reads=[b_vec], writes=[b_vec])
            P.flush()

        with ExitStack() as es:
            wa = sb(es, nc, "wa", [128, 8, NA], BF16)
            bwa = Buf()
            load_w_bf16(wa, w_a.ap()[l], NA, bwa)
            ropet = sb(es, nc, "ropet", [128, TT, 64], F32)
            brope = Buf()
            for j0 in range(0, TT, 8):
                j1 = min(TT, j0 + 8)
                P.dma("sp", lambda e, j0=j0, j1=j1: e.dma_start(out=ropet[:, j0:j1, :], in_=rope.ap()[j0 * 128:j1 * 128, :].rearrange("(j p) d -> p j d", p=128)), writes=[brope])
            x_rot = Rot(es, nc, "xa", [128, 8, 512], F32, 2)
            h_rot = Rot(es, nc, "ha", [128, 8, 512], BF16, 2)
            h_bufs = [bufs(8) for _ in range(2)]
            sq_rot = Rot(es, nc, "sqa", [128, 8, 512], BF16, 1)
            sq_bufs = [bufs(2)]
            rs_rot = Rot(es, nc, "rsa", [128, 512], F32, 1)
            tmp_rot = Rot(es, nc, "tma", [128, 512], F32, 2)
            ps_rot = Rot(es, nc, "psa", [128, 512], F32, 7, psum=True)
            pst = pstile(es, nc, "psta", [128, 8, 128], BF16)
            bpst = bufs(2)
            lx_rot = Rot(es, nc, "lxst", [128, 4, 512], F32, 1)
            lg_rot = Rot(es, nc, "lgst", [128, 4, 512], F32, 1)
            gl_rot = Rot(es, nc, "glt", [128, 4, 512], F32, 1)
            qT_rot = Rot(es, nc, "qTst", [128, 4, 512], BF16, 1)
            kT_rot = Rot(es, nc, "kTst", [128, 2, 512], BF16, 1)
            dqT_rot = Rot(es, nc, "dqTst", [128, 4, 512], BF16, 1)
            dkT_rot = Rot(es, nc, "dkTst", [128, 4, 512], BF16, 1)
            v_rot = Rot(es, nc, "vst", [128, 4, 2, 128], BF16, 1)
            dv_rot = Rot(es, nc, "dvst", [128, 4, 512], BF16, 1)
            zf_rot = Rot(es, nc, "zf", [128, 512], F32, 3)
            zt_rot = Rot(es, nc, "zt", [128, 4, 256], F32, 2)
            zb_rot = Rot(es, nc, "zb", [128, 512], BF16, 3)
            ssq = Rot(es, nc, "ssq", [128, 16], F32, 3)
            for vt in v_rot.tiles:
                P.op("pool", lambda e, vt=vt: e.memset(vt[:], 1.0), writes=v_rot.bufs)

            class _SqRot:
                def __init__(self):
                    self.r = sq_rot

                def next(self):
                    t, _ = self.r.next()
                    return t, sq_bufs[0]

            def rope_apply(zn, bzn, H, j, outb, bo):
                W = H * 64

                def v4(t, off, pstride):
                    return bass.AP(t, off, [[pstride, 128], [64, H], [32, 2], [1, 16]])

                def tab(off):
                    return bass.AP(ropet, j * 64 + off, [[TT * 64, 128], [0, H], [16, 2], [1, 16]])

                tt, btt = zt_rot.next()

                def tv(k):
                    return bass.AP(tt, k * 256, [[1024, 128], [32, H], [16, 2], [1, 16]])
                P.op("dve", lambda e: e.tensor_tensor(out=tv(0), in0=v4(zn, 0, 512), in1=tab(0), op=ALU.mult), reads=[bzn, brope], writes=[btt])
                P.op("pool", lambda e: e.tensor_tensor(out=tv(1), in0=v4(zn, 16, 512), in1=tab(32), op=ALU.mult), reads=[bzn, brope], writes=[btt])
                P.op("pool", lambda e: e.tensor_tensor(out=tv(2), in0=v4(zn, 0, 512), in1=tab(32), op=ALU.mult), reads=[bzn, brope], writes=[btt])
                P.op("dve", lambda e: e.tensor_tensor(out=tv(3), in0=v4(zn, 16, 512), in1=tab(0), op=ALU.mult), reads=[bzn, brope], writes=[btt])
                P.op("dve", lambda e: e.tensor_tensor(out=v4(outb, 0, 512), in0=tv(0), in1=tv(1), op=ALU.subtract), reads=[btt], writes=[bo])
                P.op("pool", lambda e: e.tensor_tensor(out=v4(outb, 16, 512), in0=tv(2), in1=tv(3), op=ALU.add), reads=[btt], writes=[bo])

            def qk_norm(ps, bps, off, H, gi, zn, bzn):
                W = H * 64
                sqt, bsqt = zf_rot.next()
                P.op("act", lambda e: e.activation(out=sqt[:, :W], in_=ps[:, off:off + W], func=AF.Square), reads=[bps], writes=[bsqt])
                s, bs = ssq.next()
                P.op("dve", lambda e: e.tensor_reduce(out=s[:, 0:H], in_=sqt[:, :W].rearrange("p (h d) -> p h d", d=64), axis=AX.X, op=ALU.add),
                     reads=[bsqt], writes=[bs])
                P.op("act", lambda e: e.activation(out=s[:, 0:H], in_=s[:, 0:H], func=AF.Sqrt, scale=1.0 / 64.0, bias=EPS), reads=[bs], writes=[bs])
                P.op("dve", lambda e: e.reciprocal(out=s[:, 0:H], in_=s[:, 0:H]), reads=[bs], writes=[bs])
                P.op("dve", lambda e: e.tensor_tensor(out=zn[:, :W].rearrange("p (h d) -> p h d", d=64),
                                                      in0=ps[:, off:off + W].rearrange("p (h d) -> p h d", d=64),
                                                      in1=bass.AP(s, 0, [[16, 128], [1, H], [0, 64]]), op=ALU.mult), reads=[bps, bs], writes=[bzn])
                P.op("pool", lambda e: e.tensor_tensor(out=zn[:, :W].rearrange("p (h d) -> p h d", d=64),
                                                       in0=zn[:, :W].rearrange("p (h d) -> p h d", d=64),
                                                       in1=bass.AP(qkgt, gi * 64, [[128, 128], [0, H], [1, 64]]), op=ALU.mult), reads=[bzn, b_vec], writes=[bzn])

            tiles = token_tiles(S, C, 512)
            pst_i = [0]
            for ti, (t0, n) in enumerate(tiles):
                col = 1 if t0 < C else 0
                nb = n // 128
                xt, bx = x_rot.next()
                P.dma("sp", lambda e, xt=xt, t0=t0, n=n: e.dma_start(out=xt[:, :, :n], in_=Xl.ap()[:, t0:t0 + n].rearrange("(c p) t -> p c t", p=128)), writes=[bx])
                hk = h_rot.i % 2
                hT, _ = h_rot.next()
                bh = h_bufs[hk]
                norm_mod(xt, [bx], n, G1, modv, col, hT, bh, _SqRot(), ps_rot, rs_rot, tmp_rot)
                lxt, blx = lx_rot.next()
                lgt, blg = lg_rot.next()
                for fc in range(8):
                    ps, bps = ps_rot.next()
                    for c in range(8):
                        P.op("pe", lambda e, ps=ps, fc=fc, c=c: e.matmul(ps[:, :n], lhsT=wa[:, c, OFF_LX + fc * 128:OFF_LX + (fc + 1) * 128], rhs=hT[:, c, :n],
                                                                        start=(c == 0), stop=(c == 7)), reads=[bwa, bh[c]], writes=[bps])
                    if fc < 4:
                        P.op("act", lambda e, ps=ps, fc=fc: e.activation(out=lxt[:, fc, :n], in_=ps[:, :n], func=AF.Identity, scale=1.0, bias=0.0), reads=[bps], writes=[blx])
                    else:
                        P.op("act", lambda e, ps=ps, fc=fc: e.activation(out=lgt[:, fc - 4, :n], in_=ps[:, :n], func=AF.Identity, scale=1.0, bias=0.0), reads=[bps], writes=[blg])
                glt, bgl = gl_rot.next()
                KG = 2.0 * math.sqrt(2.0 / math.pi)
                P.op("pool", lambda e: e.tensor_tensor(out=glt[:, :, :n], in0=lgt[:, :, :n], in1=lgt[:, :, :n], op=ALU.mult), reads=[blg], writes=[bgl])
                P.op("dve", lambda e: e.tensor_scalar(out=glt[:, :, :n], in0=glt[:, :, :n], scalar1=0.044715 * KG, scalar2=KG, op0=ALU.mult, op1=ALU.add), reads=[bgl], writes=[bgl])
                P.op("pool", lambda e: e.tensor_tensor(out=glt[:, :, :n], in0=glt[:, :, :n], in1=lgt[:, :, :n], op=ALU.mult), reads=[bgl, blg], writes=[bgl])
                P.op("act", lambda e: e.activation(out=glt[:, :, :n], in_=glt[:, :, :n], func=AF.Sigmoid), reads=[bgl], writes=[bgl])
                P.op("dve", lambda e: e.tensor_tensor(out=lgt[:, :, :n], in0=glt[:, :, :n], in1=lgt[:, :, :n], op=ALU.mult), reads=[bgl, blg], writes=[blg])
                P.dma("pool", lambda e, lxt=lxt, t0=t0, n=n: e.dma_start(out=lxs.ap()[:, t0:t0 + n].rearrange("(c p) t -> p c t", p=128), in_=lxt[:, :, :n]), reads=[blx])
                P.dma("pool", lambda e, lgt=lgt, t0=t0, n=n: e.dma_start(out=lgs.ap()[:, t0:t0 + n].rearrange("(c p) t -> p c t", p=128), in_=lgt[:, :, :n]), reads=[blg])
                qTt, bqT = qT_rot.next()
                kTt, bkT = kT_rot.next()
                dqTt, bdqT = dqT_rot.next()
                dkTt, bdkT = dkT_rot.next()
                vt, bv = v_rot.next()
                dvt, bdv = dv_rot.next()
                for tb in range(nb):
                    j = t0 // 128 + tb

                    def tok_mm(off, W):
                        ps, bps = ps_rot.next()
                        for c in range(8):
                            P.op("pe", lambda e, ps=ps, c=c: e.matmul(ps[:, :W], lhsT=hT[:, c, tb * 128:(tb + 1) * 128], rhs=wa[:, c, off:off + W],
                                                                     start=(c == 0), stop=(c == 7)), reads=[bwa, bh[c]], writes=[bps])
                        return ps, bps

                    def transp(src, bsrc, nch, dst, bdst):
                        k = pst_i[0] % 2
                        pst_i[0] += 1
                        for q in range(nch):
                            P.op("pe", lambda e, q=q: e.transpose(pst[:, k * 4 + q, :], src[:, q * 128:(q + 1) * 128], ident_bf[:]), reads=[bsrc, b_const], writes=[bpst[k]])
                        P.op("act", lambda e: e.copy(out=dst[:, 0:nch, tb * 128:(tb + 1) * 128], in_=pst[:, k * 4:k * 4 + nch, :]), reads=[bpst[k]], writes=[bdst])

                    ps, bps = tok_mm(OFF_GQ, 512)
                    zn, bzn = zf_rot.next()
                    qk_norm(ps, bps, 0, 8, 0, zn, bzn)
                    zb, bzb = zb_rot.next()
                    rope_apply(zn, bzn, 8, j, zb, bzb)
                    transp(zb, bzb, 4, qTt, bqT)
                    ps, bps = tok_mm(OFF_GK, 384)
                    zn, bzn = zf_rot.next()
                    qk_norm(ps, bps, 0, 4, 1, zn, bzn)
                    zb, bzb = zb_rot.next()
                    rope_apply(zn, bzn, 4, j, zb, bzb)
                    transp(zb, bzb, 2, kTt, bkT)
                    P.op("act", lambda e, ps=ps, vt=vt: e.copy(out=vt[:, tb, :, 0:64], in_=ps[:, 256:384].rearrange("p (g d) -> p g d", d=64)), reads=[bps], writes=[bv])
                    for (off, dstT, bdstT) in ((OFF_DQ, dqTt, bdqT), (OFF_DK, dkTt, bdkT)):
                        ps, bps = tok_mm(off, 512)
                        zn, bzn = zf_rot.next()
                        P.op("act", lambda e, ps=ps, zn=zn: e.copy(out=zn[:], in_=ps[:]), reads=[bps], writes=[bzn])
                        zb, bzb = zb_rot.next()
                        rope_apply(zn, bzn, 8, j, zb, bzb)
                        transp(zb, bzb, 4, dstT, bdstT)
                    ps, bps = tok_mm(OFF_DV, 512)
                    P.op("act", lambda e, ps=ps, dvt=dvt: e.copy(out=dvt[:, tb, :], in_=ps[:]), reads=[bps], writes=[bdv])
                P.dma("pool", lambda e, qTt=qTt, t0=t0, n=n: e.dma_start(out=qgs.ap()[:, :, t0:t0 + n].rearrange("h r t -> r h t"), in_=qTt[:, :, :n]), reads=[bqT])
                P.dma("pool", lambda e, kTt=kTt, t0=t0, n=n: e.dma_start(out=kgs.ap()[:, :, t0:t0 + n].rearrange("h r t -> r h t"), in_=kTt[:, :, :n]), reads=[bkT])
                P.dma("pool", lambda e, dqTt=dqTt, t0=t0, n=n: e.dma_start(out=dqs.ap()[:, :, t0:t0 + n].rearrange("h r t -> r h t"), in_=dqTt[:, :, :n]), reads=[bdqT])
                P.dma("pool", lambda e, dkTt=dkTt, t0=t0, n=n: e.dma_start(out=dks.ap()[:, :, t0:t0 + n].rearrange("h r t -> r h t"), in_=dkTt[:, :, :n]), reads=[bdkT])
                P.dma("pool", lambda e, vt=vt, t0=t0, n=n, nb=nb: e.dma_start(out=vgs.ap()[t0:t0 + n, :].rearrange("(b p) x -> p b x", p=128),
                                                                             in_=vt[:, 0:nb].rearrange("p b g d -> p b (g d)")), reads=[bv])
                P.dma("pool", lambda e, dvt=dvt, t0=t0, n=n, nb=nb: e.dma_start(out=dvs.ap()[t0:t0 + n, :].rearrange("(b p) x -> p b x", p=128), in_=dvt[:, 0:nb, :]), reads=[bdv])
            P.flush()

        with ExitStack() as es:
            wbd = sb(es, nc, "wbd", [128, 2, 2, 128], F32)
            bwbd = Buf()
            Xa = sb(es, nc, "Xa", [128, T], F32)
            Ua = sb(es, nc, "Ua", [128, T], F32)
            H0 = sb(es, nc, "H0", [128, T], F32)
            bX, bU, bH0 = Buf(), Buf(), Buf()
            ps_rot = Rot(es, nc, "psb", [128, 512], F32, 4, psum=True)
            r_rot = Rot(es, nc, "rb", [128, 512], F32, 2)
            i_rot = Rot(es, nc, "ib", [128, 512], F32, 2)
            a_rot = Rot(es, nc, "ab", [128, 512], F32, 2)
            s_rot = Rot(es, nc, "sb_", [128, 512], F32, 2)
            hb_rot = Rot(es, nc, "hb", [128, 512], F32, 3)
            g_rot = Rot(es, nc, "gb", [128, 512], F32, 2)
            y_rot = Rot(es, nc, "yb", [128, 512], BF16, 2)
            segs = [(0, C), (C, T)]
            tiles = token_tiles(S, C, 512)
            for ch in range(4):
                for d in range(2):
                    P.dma("sp", lambda e, d=d, ch=ch: e.dma_start(out=wbd[:, d, 0, :], in_=w_rgbd.ap()[l, d, ch]), writes=[bwbd])
                    P.dma("sp", lambda e, d=d, ch=ch: e.dma_start(out=wbd[:, d, 1, :], in_=w_igbd.ap()[l, d, ch]), writes=[bwbd])
                P.dma("sp", lambda e, ch=ch: e.dma_start(out=Xa[:], in_=lxs.ap()[ch * 128:(ch + 1) * 128, :]), writes=[bX])
                for (s0, s1) in segs:
                    for p0 in range(s0, s1, 2048):
                        p1 = min(p0 + 2048, s1)
                        P.op("dve", lambda e, p0=p0, p1=p1, ch=ch: e.tensor_scalar(out=Ua[:, p0:p1], in0=Xa[:, p0:p1], scalar1=cvt[:, ch, 2:3], scalar2=cvt[:, ch, 4:5],
                                                                                 op0=ALU.mult, op1=ALU.add), reads=[bX, b_vec], writes=[bU])
                    for p0 in range(s0, s1, 2048):
                        p1 = min(p0 + 2048, s1)
                        for (tap, sh) in ((0, -2), (1, -1), (3, 1)):
                            o0 = max(p0, s0 - sh) if sh < 0 else p0
                            o1 = p1 if sh < 0 else min(p1, s1 - sh)
                            P.op("dve", lambda e, o0=o0, o1=o1, sh=sh, tap=tap, ch=ch: e.scalar_tensor_tensor(
                                out=Ua[:, o0:o1], in0=Xa[:, o0 + sh:o1 + sh], scalar=cvt[:, ch, tap:tap + 1], in1=Ua[:, o0:o1], op0=ALU.mult, op1=ALU.add),
                                reads=[bX, bU, b_vec], writes=[bU])

                def coeffs(d, t0, n):
                    psr, bpsr = ps_rot.next()
                    psi, bpsi = ps_rot.next()
                    P.op("pe", lambda e: e.matmul(psr[:, :n], lhsT=wbd[:, d, 0, :], rhs=Ua[:, t0:t0 + n], start=True, stop=True), reads=[bwbd, bU], writes=[bpsr])
                    P.op("pe", lambda e: e.matmul(psi[:, :n], lhsT=wbd[:, d, 1, :], rhs=Ua[:, t0:t0 + n], start=True, stop=True), reads=[bwbd, bU], writes=[bpsi])
                    r, br = r_rot.next()
                    ii, bi = i_rot.next()
                    P.op("act", lambda e: e.activation(out=r[:, :n], in_=psr[:, :n], func=AF.Sigmoid, bias=lvt[:, d, ch, 0:1], scale=1.0), reads=[bpsr, b_vec], writes=[br])
                    P.op("act", lambda e: e.activation(out=ii[:, :n], in_=psi[:, :n], func=AF.Sigmoid, bias=lvt[:, d, ch, 1:2], scale=1.0), reads=[bpsi, b_vec], writes=[bi])
                    a, ba = a_rot.next()
                    P.op("act", lambda e: e.activation(out=a[:, :n], in_=r[:, :n], func=AF.Exp, scale=cch[:, d, ch:ch + 1]), reads=[br, b_vec], writes=[ba])
                    s, bs = s_rot.next()
                    P.op("pool", lambda e: e.tensor_tensor(out=s[:, :n], in0=a[:, :n], in1=a[:, :n], op=ALU.mult), reads=[ba], writes=[bs])
                    P.op("act", lambda e: e.activation(out=s[:, :n], in_=s[:, :n], func=AF.Sqrt, scale=-1.0, bias=1.0), reads=[bs], writes=[bs])
                    P.op("dve", lambda e: e.tensor_tensor(out=ii[:, :n], in0=ii[:, :n], in1=Ua[:, t0:t0 + n], op=ALU.mult), reads=[bi, bU], writes=[bi])
                    P.op("dve", lambda e: e.tensor_tensor(out=ii[:, :n], in0=ii[:, :n], in1=s[:, :n], op=ALU.mult), reads=[bi, bs], writes=[bi])
                    return a, ba, ii, bi

                for ti, (t0, n) in enumerate(tiles):
                    a, ba, b_, bb = coeffs(0, t0, n)
                    init = 0.0 if ti == 0 else H0[:, t0 - 1:t0]
                    P.op("dve", lambda e, a=a, b_=b_, t0=t0, n=n, init=init: e.tensor_tensor_scan(out=H0[:, t0:t0 + n], data0=a[:, :n], data1=b_[:, :n], initial=init,
                                                                                                  op0=ALU.mult, op1=ALU.add), reads=[ba, bb, bH0], writes=[bH0])
                ctx_tiles = [tt_ for tt_ in tiles if tt_[0] < C]
                lat_tiles = [tt_ for tt_ in tiles if tt_[0] >= C]
                order = list(reversed(ctx_tiles)) + list(reversed(lat_tiles))
                prev = None
                for (t0, n) in order:
                    a, ba, b_, bb = coeffs(1, t0, n)
                    hb, bhb = hb_rot.next()
                    init = 0.0 if prev is None else prev[0][:, 0:1]
                    rd = ([prev[1]] if prev is not None else [])

                    def rev(t, n=n):
                        return bass.AP(t, n - 1, [[512, 128], [-1, n]])
                    P.op("dve", lambda e, a=a, b_=b_, hb=hb, init=init, rev=rev: e.tensor_tensor_scan(out=rev(hb), data0=rev(a), data1=rev(b_), initial=init,
                                                                                                   op0=ALU.mult, op1=ALU.add), reads=[ba, bb] + rd, writes=[bhb])
                    prev = (hb, bhb)
                    g, bg = g_rot.next()
                    P.dma("sp", lambda e, g=g, t0=t0, n=n, ch=ch: e.dma_start(out=g[:, :n], in_=lgs.ap()[ch * 128:(ch + 1) * 128, t0:t0 + n]), writes=[bg])
                    y, by = y_rot.next()
                    t2, bt2 = s_rot.next()
                    P.op("pool", lambda e, t2=t2, hb=hb, t0=t0, n=n: e.tensor_tensor(out=t2[:, :n], in0=hb[:, :n], in1=H0[:, t0:t0 + n], op=ALU.add), reads=[bhb, bH0], writes=[bt2])
                    P.op("dve", lambda e, y=y, t2=t2, g=g, n=n: e.tensor_tensor(out=y[:, :n], in0=t2[:, :n], in1=g[:, :n], op=ALU.mult), reads=[bt2, bg], writes=[by])
                    P.dma("pool", lambda e, y=y, t0=t0, n=n, ch=ch: e.dma_start(out=yrs.ap()[ch * 128:(ch + 1) * 128, t0:t0 + n], in_=y[:, :n]), reads=[by])
            P.flush()

        qtiles = token_tiles(S, C, 512, with_ctx=not last)

        def run_pipeline(groups):
            deferred = []
            ng = len(groups)
            if ng == 0:
                return
            groups[0]["pre"]()
            groups[0]["S"]()
            for i in range(ng):
                if i + 1 < ng:
                    groups[i + 1]["pre"]()
                    groups[i + 1]["S"]()
                groups[i]["exp"]()
                groups[i]["PV"]()
                for (dly, fn) in groups[i]["post"]():
                    deferred.append((i + dly, fn))
                ready = [d for d in deferred if d[0] <= i]
                deferred[:] = [d for d in deferred if d[0] > i]
                for d in ready:
                    d[1]()
            while deferred:
                deferred.pop(0)[1]()

        def exp_op(Sp, bS, Pt, bP, n):
            if n == 512:
                P.op("act", lambda e: e.activation(out=Pt[:], in_=Sp[:], func=AF.Exp, scale=0.125), reads=[bS], writes=[bP])
            else:
                P.op("act", lambda e: e.activation(out=Pt[:].rearrange("p (a q) -> p a q", a=2)[:, :, 0:n],
                                                   in_=Sp[:].rearrange("p (a q) -> p a q", a=2)[:, :, 0:n], func=AF.Exp, scale=0.125), reads=[bS], writes=[bP])

        with ExitStack() as es:
            psS = Rot(es, nc, "psS", [128, 1024], F32, 2, psum=True)
            psO = Rot(es, nc, "psO", [128, 512], F32, 4, psum=True)
            kTs = sb(es, nc, "kTs", [128, 2, T], BF16)
            vA = sb(es, nc, "vA", [128, TT, 256], BF16)
            bk, bvA = Buf(), Buf()
            for g in range(2):
                P.dma("sp", lambda e, g=g: e.dma_start(out=kTs[:, g, :], in_=kgs.ap()[g]), writes=[bk])
            for j0 in range(0, TT, 8):
                j1 = min(TT, j0 + 8)
                P.dma("sp", lambda e, j0=j0, j1=j1: e.dma_start(out=vA[:, j0:j1, :], in_=vgs.ap()[j0 * 128:j1 * 128, :].rearrange("(j p) x -> p j x", p=128)), writes=[bvA])
            q_rot = Rot(es, nc, "qc", [128, 512], BF16, 3)
            p_rot = Rot(es, nc, "pc", [128, 1024], BF16, 3)
            o_rot = Rot(es, nc, "oc", [128, 512], F32, 4)
            z_rot = Rot(es, nc, "zc", [64, 512], F32, 4)
            y_rot = Rot(es, nc, "yc", [64, 512], BF16, 4)
            groups = []

            def make_head_c1(t0, n, p):
                g = p // 2
                nkt = (C // 128) if t0 < C else TT
                st = {}

                def pre0():
                    st["qT"], st["bq"] = q_rot.next()
                    qT = st["qT"]
                    P.dma("sp", lambda e: e.dma_start(out=qT[:, :n], in_=qgs.ap()[p, :, t0:t0 + n]), writes=[st["bq"]])
                    st["OA"], st["bOA"] = psO.next()
                    st["OB"], st["bOB"] = psO.next()

                def mk(kt):
                    gs = {}

                    def pre():
                        if kt == 0:
                            pre0()

                    def S_():
                        Sp, bS = psS.next()
                        gs["Sp"], gs["bS"] = Sp, bS
                        qT, bq = st["qT"], st["bq"]
                        P.op("pe", lambda e: e.matmul(Sp[:, 0:n], lhsT=kTs[0:64, g, kt * 128:(kt + 1) * 128], rhs=qT[0:64, :n], start=True, stop=True), reads=[bk, bq], writes=[bS])
                        P.op("pe", lambda e: e.matmul(Sp[:, 512:512 + n], lhsT=kTs[64:128, g, kt * 128:(kt + 1) * 128], rhs=qT[64:128, :n], start=True, stop=True), reads=[bk, bq], writes=[bS])

                    def exp_():
                        Pt, bP = p_rot.next()
                        gs["Pt"], gs["bP"] = Pt, bP
                        exp_op(gs["Sp"], gs["bS"], Pt, bP, n)

                    def PV_():
                        Pt, bP = gs["Pt"], gs["bP"]
                        OA, OB = st["OA"], st["OB"]
                        P.op("pe", lambda e: e.matmul(OA[:, :n], lhsT=vA[:, kt, g * 128:(g + 1) * 128], rhs=Pt[:, 0:n], start=(kt == 0), stop=(kt == nkt - 1)), reads=[bvA, bP], writes=[st["bOA"]])
                        P.op("pe", lambda e: e.matmul(OB[:, :n], lhsT=vA[:, kt, g * 128:(g + 1) * 128], rhs=Pt[:, 512:512 + n], start=(kt == 0), stop=(kt == nkt - 1)), reads=[bvA, bP], writes=[st["bOB"]])

                    def post():
                        if kt != nkt - 1:
                            return []
                        for hi, (Ox, bOx) in enumerate(((st["OA"], st["bOA"]), (st["OB"], st["bOB"]))):
                            h = 2 * p + hi
                            osb, bosb = o_rot.next()
                            P.op("dve", lambda e, osb=osb, Ox=Ox: e.tensor_copy(out=osb[:, :n], in_=Ox[:, :n]), reads=[bOx], writes=[bosb])
                            zt, bzt = z_rot.next()
                            P.dma("sp", lambda e, zt=zt, osb=osb: e.dma_start(out=zt[:, :n], in_=osb[64:128, :n]), reads=[bosb], writes=[bzt])
                            P.op("dve", lambda e, zt=zt: e.reciprocal(out=zt[:, :n], in_=zt[:, :n]), reads=[bzt], writes=[bzt])
                            yt, byt = y_rot.next()
                            P.op("dve", lambda e, yt=yt, osb=osb, zt=zt: e.tensor_tensor(out=yt[:, :n], in0=osb[0:64, :n], in1=zt[:, :n], op=ALU.mult), reads=[bosb, bzt], writes=[byt])
                            P.dma("pool", lambda e, yt=yt, h=h: e.dma_start(out=yas.ap()[h * 64:(h + 1) * 64, t0:t0 + n], in_=yt[:, :n]), reads=[byt])
                        return []
                    return {"pre": pre, "S": S_, "exp": exp_, "PV": PV_, "post": post}
                for kt in range(nkt):
                    groups.append(mk(kt))

            for (t0, n) in qtiles:
                for p in range(4):
                    make_head_c1(t0, n, p)
            run_pipeline(groups)
            P.flush()

        with ExitStack() as es:
            psS = Rot(es, nc, "psS2", [128, 1024], F32, 2, psum=True)
            psO = Rot(es, nc, "psO2", [128, 512], F32, 4, psum=True)
            dkT = sb(es, nc, "dkT", [128, 4, T], BF16)
            dvA = sb(es, nc, "dvA", [128, TT, 512], BF16)
            bk, bvA = Buf(), Buf()
            for h in range(4):
                P.dma("sp", lambda e, h=h: e.dma_start(out=dkT[:, h, :], in_=dks.ap()[h]), writes=[bk])
            for j0 in range(0, TT, 8):
                j1 = min(TT, j0 + 8)
                P.dma("sp", lambda e, j0=j0, j1=j1: e.dma_start(out=dvA[:, j0:j1, :], in_=dvs.ap()[j0 * 128:j1 * 128, :].rearrange("(j p) x -> p j x", p=128)), writes=[bvA])
            q_rot = Rot(es, nc, "qd", [128, 512], BF16, 3)
            p_rot = Rot(es, nc, "pd", [128, 1024], BF16, 3)
            r_rot = Rot(es, nc, "rd", [128, 512], F32, 4)
            o_rot = Rot(es, nc, "od", [128, 512], F32, 2)
            t_rot = Rot(es, nc, "td", [128, 512], F32, 2)
            sq_rot = Rot(es, nc, "sqd", [128, 512], BF16, 2)
            y_rot = Rot(es, nc, "yd", [128, 512], BF16, 2)
            groups = []

            acc_rot = Rot(es, nc, "accd", [128, 1024], F32, 4)
            acc_pb = [[Buf(), Buf()] for _ in range(4)]

            def make_head_c2(t0, n, h):
                nkt = (C // 128) if t0 < C else TT
                st = {}

                def pre0():
                    st["qT"], st["bq"] = q_rot.next()
                    qT = st["qT"]
                    P.dma("sp", lambda e: e.dma_start(out=qT[:, :n], in_=dqs.ap()[h, :, t0:t0 + n]), writes=[st["bq"]])
                    for nm in ("O1", "O2"):
                        st[nm], st["b" + nm] = psO.next()
                    st["acc"], st["accb"] = [], []
                    for _k in range(2):
                        st["accb"].append(acc_pb[acc_rot.i % 4])
                        st["acc"].append(acc_rot.next())

                def mk(kt):
                    gs = {}

                    def pre():
                        if kt == 0:
                            pre0()

                    def S_():
                        Sp, bS = psS.next()
                        gs["Sp"], gs["bS"] = Sp, bS
                        qT, bq = st["qT"], st["bq"]
                        P.op("pe", lambda e: e.matmul(Sp[:, 0:n], lhsT=dkT[0:64, h, kt * 128:(kt + 1) * 128], rhs=qT[0:64, :n], start=True, stop=True), reads=[bk, bq], writes=[bS])
                        P.op("pe", lambda e: e.matmul(Sp[:, 512:512 + n], lhsT=dkT[64:128, h, kt * 128:(kt + 1) * 128], rhs=qT[64:128, :n], start=True, stop=True), reads=[bk, bq], writes=[bS])

                    def exp_():
                        Pt, bP = p_rot.next()
                        gs["Pt"], gs["bP"] = Pt, bP
                        exp_op(gs["Sp"], gs["bS"], Pt, bP, n)

                    def PV_():
                        Pt, bP = gs["Pt"], gs["bP"]
                        s0, s1 = (kt == 0), (kt == nkt - 1)
                        O1, O2 = st["O1"], st["O2"]
                        P.op("pe", lambda e: e.matmul(O1[:, :n], lhsT=dvA[:, kt, h * 128:(h + 1) * 128], rhs=Pt[:, 0:n], start=s0, stop=s1), reads=[bvA, bP], writes=[st["bO1"]])
                        P.op("pe", lambda e: e.matmul(O2[:, :n], lhsT=dvA[:, kt, h * 128:(h + 1) * 128], rhs=Pt[:, 512:512 + n], start=s0, stop=s1), reads=[bvA, bP], writes=[st["bO2"]])
                        (acc, _), ab = st["acc"][kt % 2], st["accb"][kt % 2]
                        parts = (("dve", 0, 1024, ab[0]),)
                        for (eng, c0, c1, bb) in parts:
                            if kt < 2:
                                P.op(eng, lambda e: e.tensor_copy(out=acc[:, c0:c1], in_=Pt[:, c0:c1]), reads=[bP], writes=[bb])
                            else:
                                P.op(eng, lambda e: e.tensor_tensor(out=acc[:, c0:c1], in0=acc[:, c0:c1], in1=Pt[:, c0:c1], op=ALU.add), reads=[bP, bb], writes=[bb])

                    def post():
                        if kt != nkt - 1:
                            return []
                        O1, O2 = st["O1"], st["O2"]
                        bO1, bO2 = st["bO1"], st["bO2"]
                        hold = {}

                        def stage1():
                            Zx, bZx = psS.next()
                            psS.i += 1
                            for half in range(2):
                                for k in range(2):
                                    acc = st["acc"][k][0]
                                    P.op("pe", lambda e, acc=acc, half=half, k=k: e.matmul(Zx[:, half * 512:half * 512 + n], lhsT=ones_f[:], rhs=acc[:, half * 512:half * 512 + n],
                                                                                        start=(k == 0), stop=(k == 1)), reads=[b_const] + st["accb"][k], writes=[bZx])
                            r1, br1 = r_rot.next()
                            r2, br2 = r_rot.next()
                            P.op("dve", lambda e: e.reciprocal(out=r1[:, :n], in_=Zx[:, 0:n]), reads=[bZx], writes=[br1])
                            P.op("dve", lambda e: e.reciprocal(out=r2[:, :n], in_=Zx[:, 512:512 + n]), reads=[bZx], writes=[br2])
                            o, bo = o_rot.next()
                            tt_, btt = t_rot.next()
                            P.op("dve", lambda e: e.tensor_tensor(out=o[:, :n], in0=O1[:, :n], in1=r1[:, :n], op=ALU.mult), reads=[bO1, br1], writes=[bo])
                            P.op("dve", lambda e: e.tensor_tensor(out=tt_[:, :n], in0=O2[:, :n], in1=r2[:, :n], op=ALU.mult), reads=[bO2, br2], writes=[btt])
                            P.op("dve", lambda e: e.scalar_tensor_tensor(out=o[:, :n], in0=tt_[:, :n], scalar=neglam[:, 0:1], in1=o[:, :n], op0=ALU.mult, op1=ALU.add),
                                 reads=[bo, btt, b_vec], writes=[bo])
                            sq, bsq = sq_rot.next()
                            P.op("pool", lambda e: e.tensor_tensor(out=sq[:, :n], in0=o[:, :n], in1=o[:, :n], op=ALU.mult), reads=[bo], writes=[bsq])
                            hold.update(o=o, bo=bo, sq=sq, bsq=bsq)

                        def stage2():
                            o, bo, sq, bsq = hold["o"], hold["bo"], hold["sq"], hold["bsq"]
                            Sx, bSx = psS.next()
                            psS.i += 1
                            P.op("pe", lambda e: e.matmul(Sx[:, :n], lhsT=ones_bf[:], rhs=sq[:, :n], start=True, stop=True), reads=[b_const, bsq], writes=[bSx])
                            r3, br3 = r_rot.next()
                            P.op("act", lambda e: e.activation(out=r3[:, :n], in_=Sx[:, :n], func=AF.Ln, bias=128.0 * EPS, scale=1.0), reads=[bSx], writes=[br3])
                            P.op("act", lambda e: e.activation(out=r3[:, :n], in_=r3[:, :n], func=AF.Exp, scale=-0.5), reads=[br3], writes=[br3])
                            yt, byt = y_rot.next()
                            P.op("dve", lambda e: e.scalar_tensor_tensor(out=yt[:, :n], in0=o[:, :n], scalar=gsub[:, 0:1], in1=r3[:, :n], op0=ALU.mult, op1=ALU.mult),
                                 reads=[bo, br3, b_vec], writes=[byt])
                            P.dma("pool", lambda e: e.dma_start(out=yds.ap()[h * 128:(h + 1) * 128, t0:t0 + n], in_=yt[:, :n]), reads=[byt])
                        return [(2, stage1), (6, stage2)]
                    return {"pre": pre, "S": S_, "exp": exp_, "PV": PV_, "post": post}
                for kt in range(nkt):
                    groups.append(mk(kt))

            for (t0, n) in qtiles:
                for h in range(4):
                    make_head_c2(t0, n, h)
            run_pipeline(groups)
            P.flush()

        dtiles = token_tiles(S, C, 512, with_ctx=not last)
        with ExitStack() as es:
            wg = sb(es, nc, "wg", [128, 8, 3072], BF16)
            wb = sb(es, nc, "wb", [128, 12, 1024], BF16)
            wo = sb(es, nc, "wo", [128, 8, 1024], BF16)
            bwg, bwb, bwo = Buf(), Buf(), Buf()
            load_w_bf16(wg, w_g.ap()[l], 3072, bwg)
            load_w_bf16(wb, w_br.ap()[l].rearrange("n k m -> (n k) m"), 1024, bwb, nchunk=12)
            load_w_bf16(wo, w_out.ap()[l], 1024, bwo)
            x_rot = Rot(es, nc, "xd", [128, 8, 512], F32, 2)
            h_rot = Rot(es, nc, "hd", [128, 8, 512], BF16, 1)
            h_bufs = [bufs(8)]
            sq_rot = Rot(es, nc, "sqd1", [128, 8, 512], BF16, 1)
            sq_bufs = [bufs(2)]
            rs_rot = Rot(es, nc, "rsd", [128, 512], F32, 1)
            tmp_rot = Rot(es, nc, "tmd", [128, 512], F32, 2)
            ps_rot = Rot(es, nc, "psd", [128, 512], F32, 8, psum=True)
            y_rot = Rot(es, nc, "yd1", [128, 12, 512], BF16, 2)
            gt_rot = Rot(es, nc, "gtd", [128, 512], F32, 3)
            mt_rot = Rot(es, nc, "mtd", [128, 512], F32, 3)
            macc = Rot(es, nc, "macc", [128, 512], F32, 2)
            mbf = sb(es, nc, "mbf", [128, 8, 512], BF16)
            bmbf = bufs(8)

            class _SqRot1:
                def next(self):
                    t, _ = sq_rot.next()
                    return t, sq_bufs[0]

            for (t0, n) in dtiles:
                col = 1 if t0 < C else 0
                xt, bx = x_rot.next()
                P.dma("sp", lambda e, xt=xt, t0=t0, n=n: e.dma_start(out=xt[:, :, :n], in_=Xl.ap()[:, t0:t0 + n].rearrange("(c p) t -> p c t", p=128)), writes=[bx])
                yt, by = y_rot.next()
                for bi_, ysrc in enumerate((yrs, yas, yds)):
                    P.dma("sp", lambda e, yt=yt, bi_=bi_, ysrc=ysrc, t0=t0, n=n: e.dma_start(out=yt[:, bi_ * 4:(bi_ + 1) * 4, :n],
                                                                                           in_=ysrc.ap()[:, t0:t0 + n].rearrange("(c p) t -> p c t", p=128)), writes=[by])
                hT, _ = h_rot.next()
                bh = h_bufs[0]
                norm_mod(xt, [bx], n, G1, modv, col, hT, bh, _SqRot1(), ps_rot, rs_rot, tmp_rot)
                for fc in range(8):
                    acc, bacc = macc.next()
                    for nb_ in range(3):
                        psg, bpsg = ps_rot.next()
                        for c in range(8):
                            P.op("pe", lambda e, psg=psg, c=c, nb_=nb_, fc=fc: e.matmul(psg[:, :n], lhsT=wg[:, c, nb_ * 1024 + fc * 128:nb_ * 1024 + (fc + 1) * 128], rhs=hT[:, c, :n],
                                                                                      start=(c == 0), stop=(c == 7)), reads=[bwg, bh[c]], writes=[bpsg])
                        gt, bgt_ = gt_rot.next()
                        P.op("act", lambda e, gt=gt, psg=psg, nb_=nb_, fc=fc: e.activation(out=gt[:, :n], in_=psg[:, :n], func=AF.Sigmoid, bias=bgt[:, nb_ * 8 + fc:nb_ * 8 + fc + 1], scale=1.0),
                             reads=[bpsg, b_vec], writes=[bgt_])
                        psp, bpsp = ps_rot.next()
                        for kc in range(4):
                            P.op("pe", lambda e, psp=psp, kc=kc, nb_=nb_, fc=fc: e.matmul(psp[:, :n], lhsT=wb[:, nb_ * 4 + kc, fc * 128:(fc + 1) * 128], rhs=yt[:, nb_ * 4 + kc, :n],
                                                                                        start=(kc == 0), stop=(kc == 3)), reads=[bwb, by], writes=[bpsp])
                        if nb_ == 0:
                            P.op("dve", lambda e, acc=acc, psp=psp, gt=gt: e.tensor_tensor(out=acc[:, :n], in0=psp[:, :n], in1=gt[:, :n], op=ALU.mult), reads=[bpsp, bgt_], writes=[bacc])
                        else:
                            mt, bmt_ = mt_rot.next()
                            P.op("dve", lambda e, mt=mt, psp=psp, gt=gt: e.tensor_tensor(out=mt[:, :n], in0=psp[:, :n], in1=gt[:, :n], op=ALU.mult), reads=[bpsp, bgt_], writes=[bmt_])
                            if nb_ == 1:
                                P.op("pool", lambda e, acc=acc, mt=mt: e.tensor_tensor(out=acc[:, :n], in0=acc[:, :n], in1=mt[:, :n], op=ALU.add), reads=[bacc, bmt_], writes=[bacc])
                            else:
                                P.op("pool", lambda e, acc=acc, mt=mt, fc=fc: e.tensor_tensor(out=mbf[:, fc, :n], in0=acc[:, :n], in1=mt[:, :n], op=ALU.add), reads=[bacc, bmt_], writes=[bmbf[fc]])
                for oc in range(8):
                    pso, bpso = ps_rot.next()
                    for fc in range(8):
                        P.op("pe", lambda e, pso=pso, fc=fc, oc=oc: e.matmul(pso[:, :n], lhsT=wo[:, fc, oc * 128:(oc + 1) * 128], rhs=mbf[:, fc, :n], start=(fc == 0), stop=(fc == 7)),
                             reads=[bwo, bmbf[fc]], writes=[bpso])
                    P.op("dve", lambda e, pso=pso, oc=oc, xt=xt, col=col: e.scalar_tensor_tensor(out=xt[:, oc, :n], in0=pso[:, :n], scalar=modv[:, 16 + oc, col:col + 1], in1=xt[:, oc, :n],
                                                                                               op0=ALU.mult, op1=ALU.add), reads=[bpso, bx, b_vec], writes=[bx])
                P.dma("pool", lambda e, xt=xt, t0=t0, n=n: e.dma_start(out=x1s.ap()[:, t0:t0 + n].rearrange("(c p) t -> p c t", p=128), in_=xt[:, :, :n]), reads=[bx])
            P.flush()

        mtiles = token_tiles(S, C, 256, with_ctx=not last)
        with ExitStack() as es:
            wu = sb(es, nc, "wu", [128, 8, 4096], BF16)
            wd = sb(es, nc, "wd", [128, 32, 1024], BF16)
            bwu, bwd = Buf(), Buf()
            load_w_bf16(wu, w_up.ap()[l], 4096, bwu)
            load_w_bf16(wd, w_down.ap()[l], 1024, bwd, nchunk=32)
            x_rot = Rot(es, nc, "xm", [128, 8, 256], F32, 2)
            h_rot = Rot(es, nc, "hm", [128, 8, 256], BF16, 1)
            h_bufs = [bufs(8)]
            sq_rot = Rot(es, nc, "sqm", [128, 8, 256], BF16, 1)
            sq_bufs = [bufs(2)]
            rs_rot = Rot(es, nc, "rsm", [128, 256], F32, 1)
            tmp_rot = Rot(es, nc, "tmm", [128, 256], F32, 2)
            ps_rot = Rot(es, nc, "psm", [128, 512], F32, 8, psum=True)
            at = sb(es, nc, "atm", [128, 32, 256], BF16)
            bat = bufs(32)
            rl_rot = Rot(es, nc, "rlm", [128, 256], F32, 3)
            o_rot = Rot(es, nc, "om", [128, 8, 256], F32, 1)

            class _SqRot2:
                def next(self):
                    t, _ = sq_rot.next()
                    return t, sq_bufs[0]

            for (t0, n) in mtiles:
                col = 1 if t0 < C else 0
                xt, bx = x_rot.next()
                P.dma("sp", lambda e, xt=xt, t0=t0, n=n: e.dma_start(out=xt[:, :, :n], in_=x1s.ap()[:, t0:t0 + n].rearrange("(c p) t -> p c t", p=128)), writes=[bx])
                hT, _ = h_rot.next()
                bh = h_bufs[0]
                norm_mod(xt, [bx], n, G2, modv[:, 24:32, :], col, hT, bh, _SqRot2(), ps_rot, rs_rot, tmp_rot)
                for fc in range(32):
                    psu, bpsu = ps_rot.next()
                    for c in range(8):
                        P.op("pe", lambda e, psu=psu, c=c, fc=fc: e.matmul(psu[:, :n], lhsT=wu[:, c, fc * 128:(fc + 1) * 128], rhs=hT[:, c, :n], start=(c == 0), stop=(c == 7)),
                             reads=[bwu, bh[c]], writes=[bpsu])
                    rl, brl = rl_rot.next()
                    P.op("act", lambda e, rl=rl, psu=psu: e.activation(out=rl[:, :n], in_=psu[:, :n], func=AF.Relu), reads=[bpsu], writes=[brl])
                    P.op("dve" if fc % 2 == 0 else "pool", lambda e, rl=rl, fc=fc: e.tensor_tensor(out=at[:, fc, :n], in0=rl[:, :n], in1=rl[:, :n], op=ALU.mult), reads=[brl], writes=[bat[fc]])
                for oc in range(8):
                    psd, bpsd = ps_rot.next()
                    for fc in range(32):
                        P.op("pe", lambda e, psd=psd, fc=fc, oc=oc: e.matmul(psd[:, :n], lhsT=wd[:, fc, oc * 128:(oc + 1) * 128], rhs=at[:, fc, :n], start=(fc == 0), stop=(fc == 31)),
                             reads=[bwd, bat[fc]], writes=[bpsd])
                    P.op("dve", lambda e, psd=psd, oc=oc, xt=xt, col=col: e.scalar_tensor_tensor(out=xt[:, oc, :n], in0=psd[:, :n], scalar=modv[:, 40 + oc, col:col + 1], in1=xt[:, oc, :n],
                                                                                               op0=ALU.mult, op1=ALU.add), reads=[bpsd, bx, b_vec], writes=[bx])
                if not last:
                    P.dma("pool", lambda e, xt=xt, t0=t0, n=n: e.dma_start(out=x2s.ap()[:, t0:t0 + n].rearrange("(c p) t -> p c t", p=128), in_=xt[:, :, :n]), reads=[bx])
                else:
                    sq, bsq = _SqRot2().next()
                    for hh in range(2):
                        P.op("act", lambda e, sq=sq, xt=xt, hh=hh: e.activation(out=sq[:, hh * 4:(hh + 1) * 4, :n], in_=xt[:, hh * 4:(hh + 1) * 4, :n], func=AF.Square), reads=[bx], writes=[bsq[hh]])
                    ps, bps = ps_rot.next()
                    for c in range(8):
                        P.op("pe", lambda e, ps=ps, sq=sq, c=c: e.matmul(ps[:, :n], lhsT=ones_bf[:], rhs=sq[:, c, :n], start=(c == 0), stop=(c == 7)), reads=[bsq[c // 4], b_const], writes=[bps])
                    rs, brs = rs_rot.next()
                    P.op("act", lambda e, rs=rs, ps=ps: e.activation(out=rs[:, :n], in_=ps[:, :n], func=AF.Sqrt, scale=1.0, bias=1024.0 * EPS), reads=[bps], writes=[brs])
                    P.op("dve", lambda e, rs=rs: e.reciprocal(out=rs[:, :n], in_=rs[:, :n]), reads=[brs], writes=[brs])
                    ot, bot = o_rot.next()
                    for c in range(8):
                        P.op("dve", lambda e, ot=ot, xt=xt, rs=rs, c=c: e.scalar_tensor_tensor(out=ot[:, c, :n], in0=xt[:, c, :n], scalar=GF[:, c:c + 1], in1=rs[:, :n],
                                                                                                                  op0=ALU.mult, op1=ALU.mult), reads=[bx, brs, b_vec], writes=[bot])
                    P.dma("pool", lambda e, ot=ot, t0=t0, n=n: e.dma_start(out=outT.ap()[:, t0 - C:t0 - C + n].rearrange("(c p) t -> p c t", p=128), in_=ot[:, :, :n]), reads=[bot])
            P.flush(final=(last))

    top.close()
    P.close()
    return nc, P


def _fm(v, n):
    return np.ascontiguousarray(np.asarray(v, np.float32).reshape(n, 128).T)


def prepare_inputs(inputs, S, C, L, n_cores=8):
    f = lambda k: np.asarray(inputs[k], np.float32)
    x, c, ctx, c_ctx = f("x"), f("c"), f("ctx"), f("c_ctx")
    B = x.shape[0]
    w_in = f("w_in")
    lx, lg = w_in[:, :, 0:512], w_in[:, :, 512:1024]
    gq = w_in[:, :, 1024:1536]
    gk = w_in[:, :, 1536:1664]
    gv = w_in[:, :, 1664:1792]
    dq = w_in[:, :, 1792:2304]
    dk = w_in[:, :, 2304:2816]
    dv = w_in[:, :, 2816:3328]
    gkd = np.concatenate([gk[:, :, 0:64], gk[:, :, 0:64], gk[:, :, 64:128], gk[:, :, 64:128]], axis=2)
    w_a = np.ascontiguousarray(np.concatenate([gq, gkd, gv, dq, dk, dv, lx, lg], axis=2))
    w_g = np.ascontiguousarray(w_in[:, :, 3328:6400])

    def bd(w):
        o = np.zeros((L, 2, 4, 128, 128), np.float32)
        for ch in range(4):
            for k in range(2):
                o[:, :, ch, k * 64:(k + 1) * 64, k * 64:(k + 1) * 64] = w[:, :, ch * 2 + k]
        return o

    conv_w, conv_b = f("conv_w"), f("conv_b")
    convp = np.zeros((L, 128, 4, 5), np.float32)
    for l in range(L):
        for j in range(4):
            convp[l, :, :, j] = _fm(conv_w[l, j], 4)
        convp[l, :, :, 4] = _fm(conv_b[l], 4)
    lruv = np.zeros((L, 128, 2, 4, 3), np.float32)
    for l in range(L):
        for d in range(2):
            lruv[l, :, d, :, 0] = _fm(f("b_rg")[l, d], 4)
            lruv[l, :, d, :, 1] = _fm(f("b_ig")[l, d], 4)
            lruv[l, :, d, :, 2] = _fm(f("lru_lambda")[l, d], 4)
    rows = S // GRID_W
    row = np.broadcast_to(np.arange(rows, dtype=np.float32)[:, None], (rows, GRID_W)).reshape(-1)
    colp = np.broadcast_to(np.arange(GRID_W, dtype=np.float32)[None, :], (rows, GRID_W)).reshape(-1)
    inv = (np.float32(10000.0) ** (-np.arange(16, dtype=np.float32) * np.float32(2.0) / np.float32(32))).astype(np.float32)
    ang = np.concatenate([row[:, None] * inv, colp[:, None] * inv], axis=-1).astype(np.float32)
    rope = np.zeros((C + S, 64), np.float32)
    rope[:C, 0:32] = 1.0
    rope[C:, 0:32] = np.cos(ang)
    rope[C:, 32:64] = np.sin(ang)
    shared = {
        "w_mod": f("w_mod"),
        "b_mod": np.stack([_fm(f("b_mod")[l], 48) for l in range(L)]),
        "g1": np.stack([_fm(f("norm1_g")[l], 8) for l in range(L)]),
        "g2": np.stack([_fm(f("norm2_g")[l], 8) for l in range(L)]),
        "gfin": _fm(f("final_g"), 8),
        "w_a": w_a, "w_g": w_g,
        "b_gate": np.stack([_fm(f("b_gate")[l], 24) for l in range(L)]),
        "convp": convp,
        "w_rgbd": bd(f("w_rg")), "w_igbd": bd(f("w_ig")),
        "lruv": lruv,
        "qkg": np.ascontiguousarray(np.stack([f("q_norm_g"), f("k_norm_g")], axis=1)),
        "lamv": np.ascontiguousarray(np.stack([f("lambda_q1"), f("lambda_k1"), f("lambda_q2"), f("lambda_k2")], axis=1)),
        "subg": np.ascontiguousarray(f("subln_g")[:, :, None]),
        "w_br": f("w_branch"), "w_out": f("w_out"), "w_up": f("w_up"), "w_down": f("w_down"),
        "rope": rope,
    }
    in_maps = []
    for core in range(n_cores):
        b = core % B
        m = dict(shared)
        m["xin"] = np.ascontiguousarray(np.concatenate([ctx[b].T, x[b].T], axis=1))
        cv = np.zeros((128, 8, 2), np.float32)
        cv[:, :, 0] = _fm(c[b], 8)
        cv[:, :, 1] = _fm(c_ctx, 8)
        m["cvec"] = cv
        in_maps.append(m)
    return in_maps


_CACHE = {}


def run(inputs, n_cores=8):
    x = np.asarray(inputs["x"])
    B, S, _ = x.shape
    C = np.asarray(inputs["ctx"]).shape[1]
    L = np.asarray(inputs["w_mod"]).shape[0]
    key = (S, C, L)
    if key not in _CACHE:
        _CACHE[key] = build_program(S, C, L)[0]
    nc = _CACHE[key]
    in_maps = prepare_inputs(inputs, S, C, L, n_cores)
    res = run_bass_kernel_spmd(nc, in_maps, core_ids=list(range(n_cores)))
    out = np.stack([np.ascontiguousarray(res.results[b]["outT"].T) for b in range(B)], axis=0)
    return out.astype(np.float32)


def kernel(**inputs):
    return run(inputs, n_cores=8)
```

```python
from contextlib import ExitStack
import math
import numpy as np
import concourse.bass as bass
import concourse.mybir as mybir
from concourse.bass_utils import run_bass_kernel_spmd

F32 = mybir.dt.float32
BF16 = mybir.dt.bfloat16
AF = mybir.ActivationFunctionType
ALU = mybir.AluOpType
AX = mybir.AxisListType

D_MODEL = 1024
DEPTH = 2
CTX_LEN = 256
GRID_W = 64
EPS = 1e-6
NA = 3456
OFF_GQ, OFF_GK, OFF_GV, OFF_DQ, OFF_DK, OFF_DV, OFF_LX, OFF_LG = 0, 512, 768, 896, 1408, 1920, 2432, 2944

ENGS = ("pe", "act", "dve", "pool", "sp")
EIDX = {e: i for i, e in enumerate(ENGS)}
NE = len(ENGS)
NDMA = 40


class Buf:
    __slots__ = ("name", "last_w", "readers")

    def __init__(self, name=""):
        self.name = name
        self.last_w = None
        self.readers = []


def bufs(n):
    return [Buf() for _ in range(n)]


class _Op:
    __slots__ = ("eng", "fn", "deps", "is_dma", "eidx", "dslot", "dval", "waits", "clock", "awaited")


class _Recorder:
    def __getattr__(self, name):
        return lambda *a, **k: (name, a, k)


_REC = _Recorder()


class Prog:
    def __init__(self, nc):
        self.nc = nc
        self._stack = []
        self.sems = []
        for e in ENGS:
            cm = nc.semaphore("s_" + e)
            self.sems.append(cm.__enter__())
            self._stack.append(cm)
        self.dsems = []
        for k in range(NDMA):
            cm = nc.semaphore("d_%d" % k)
            self.dsems.append(cm.__enter__())
            self._stack.append(cm)
        self.ops = []
        self.n_eng = [0] * NE
        self.sig_cnt = [0] * NE
        self.sigmap = [dict() for _ in range(NE)]
        self.last_sig = [0] * NE
        self.n_dma = 0
        self.dma_val = [0] * NDMA
        self.dma_last_op = [None] * NDMA
        self.known = [[0] * (NE + NDMA) for _ in range(NE)]
        self.first_flush = True
        self.base = 0
        self.n_instr = 0

    def close(self):
        for cm in reversed(self._stack):
            cm.__exit__(None, None, None)

    def _rec(self, eng, fn, reads, writes, is_dma):
        o = _Op()
        o.eng = EIDX[eng]
        o.fn = fn(_REC)
        o.is_dma = is_dma
        deps = set()
        oid = self.base + len(self.ops)
        for b in reads:
            if b.last_w is not None:
                deps.add(b.last_w)
        for b in writes:
            if b.last_w is not None:
                deps.add(b.last_w)
            deps.update(b.readers)
        for b in reads:
            b.readers.append(oid)
        for b in writes:
            b.last_w = oid
            b.readers = []
        deps.discard(oid)
        o.deps = deps
        o.awaited = False
        self.ops.append(o)
        return oid

    def op(self, eng, fn, reads=(), writes=()):
        return self._rec(eng, fn, reads, writes, False)

    def dma(self, eng, fn, reads=(), writes=()):
        return self._rec(eng, fn, reads, writes, True)

    def flush(self, final=False):
        ops = self.ops
        known = self.known
        bar = [0] * (NE + NDMA)
        for e in range(NE):
            bar[e] = self.n_eng[e]
        for k in range(NDMA):
            bar[NE + k] = self.dma_val[k]
        bar_waits = [[] for _ in range(NE)]
        if not self.first_flush:
            for e in range(NE):
                for e2 in range(NE):
                    if known[e][e2] < bar[e2]:
                        bar_waits[e].append(("e", e2, self.last_sig[e2]))
                for k in range(NDMA):
                    if known[e][NE + k] < bar[NE + k]:
                        bar_waits[e].append(("d", k, bar[NE + k]))
                known[e] = list(bar)
        self.first_flush = False
        base = self.base
        NV = NE + NDMA
        for oi, o in enumerate(ops):
            oid = base + oi
            E = o.eng
            kn = known[E]
            need_e = {}
            need_d = {}
            deps = o.deps
            if o.is_dma:
                k = self.n_dma % NDMA
                self.n_dma += 1
                o.dslot = k
                o.dval = self.dma_val[k] + 16
                prev = self.dma_last_op[k]
                if prev is not None and prev >= base:
                    deps = set(deps)
                    deps.add(prev)
                self.dma_val[k] = o.dval
                self.dma_last_op[k] = oid
            for d in deps:
                if d < base:
                    continue
                od = ops[d - base]
                if od.is_dma:
                    if kn[NE + od.dslot] < od.dval:
                        if need_d.get(od.dslot, (0, None))[0] < od.dval:
                            need_d[od.dslot] = (od.dval, d)
                else:
                    E2 = od.eng
                    if E2 == E and E == 0 and not o.is_dma:
                        continue
                    if kn[E2] < od.eidx + 1:
                        if need_e.get(E2, (0, None))[0] < od.eidx + 1:
                            need_e[E2] = (od.eidx + 1, d)
            waits = []
            for E2, (v, d) in need_e.items():
                ops[d - base].awaited = True
                waits.append(("e", E2, ops[d - base].eidx, d))
            for k, (v, d) in need_d.items():
                waits.append(("d", k, v, d))
            for w in waits:
                ck = ops[w[3] - base].clock
                for i in range(NV):
                    if ck[i] > kn[i]:
                        kn[i] = ck[i]
            o.waits = waits
            ck = list(kn)
            if o.is_dma:
                ck[NE + o.dslot] = o.dval
                o.eidx = -1
            else:
                o.eidx = self.n_eng[E]
                self.n_eng[E] += 1
                ck[E] = o.eidx + 1
            o.clock = ck
        last = [None] * NE
        for o in ops:
            if not o.is_dma:
                last[o.eng] = o
        for o in last:
            if o is not None:
                o.awaited = True
        for o in ops:
            if not o.is_dma and o.awaited:
                self.sig_cnt[o.eng] += 1
                self.sigmap[o.eng][o.eidx] = self.sig_cnt[o.eng]
        for e in range(NE):
            self.last_sig[e] = self.sig_cnt[e]
        self._emit(ops, bar_waits, final)
        self.base += len(ops)
        self.ops = []

    def _emit(self, ops, bar_waits, final):
        nc = self.nc
        per = [[] for _ in range(NE)]
        for o in ops:
            per[o.eng].append(o)
        sems = self.sems
        dsems = self.dsems
        sigmap = self.sigmap
        cnt = [0]

        def run(E, eng):
            for w in bar_waits[E]:
                if w[0] == "e":
                    if w[2] > 0:
                        eng.wait_ge(sems[w[1]], w[2])
                        cnt[0] += 1
                else:
                    eng.wait_ge(dsems[w[1]], w[2])
                    cnt[0] += 1
            for o in per[E]:
                for w in o.waits:
                    if w[0] == "e":
                        eng.wait_ge(sems[w[1]], sigmap[w[1]][w[2]])
                    else:
                        eng.wait_ge(dsems[w[1]], w[2])
                    cnt[0] += 1
                ins = getattr(eng, o.fn[0])(*o.fn[1], **o.fn[2])
                o.fn = None
                cnt[0] += 1
                if o.is_dma:
                    ins.then_inc(dsems[o.dslot], 16)
                elif o.awaited:
                    ins.then_inc(sems[E], 1)
            if final:
                for k in range(NDMA):
                    if self.dma_val[k] > 0:
                        eng.wait_ge(dsems[k], self.dma_val[k])

        with nc.Block() as block:
            @block.tensor
            def _(e):
                run(0, e)

            @block.scalar
            def _(e):
                run(1, e)

            @block.vector
            def _(e):
                run(2, e)

            @block.gpsimd
            def _(e):
                run(3, e)

            @block.sync
            def _(e):
                run(4, e)
        self.n_instr += cnt[0]


_UID = [0]


def _uname(name):
    _UID[0] += 1
    return "%s_%d" % (name, _UID[0])


class Rot:
    def __init__(self, es, nc, name, shape, dtype, n, psum=False):
        self.tiles = []
        self.bufs = []
        for i in range(n):
            mk = nc.psum_tensor if psum else nc.sbuf_tensor
            self.tiles.append(es.enter_context(mk(_uname("%s%d" % (name, i)), shape, dtype)))
            self.bufs.append(Buf())
        self.i = 0

    def next(self):
        k = self.i % len(self.tiles)
        self.i += 1
        return self.tiles[k], self.bufs[k]


def sb(es, nc, name, shape, dtype):
    return es.enter_context(nc.sbuf_tensor(_uname(name), shape, dtype))


def pstile(es, nc, name, shape, dtype):
    return es.enter_context(nc.psum_tensor(_uname(name), shape, dtype))


def token_tiles(S, C, width, with_ctx=True):
    tl = []
    if with_ctx:
        for t0 in range(0, C, width):
            tl.append((t0, min(width, C - t0)))
    for t0 in range(C, C + S, width):
        tl.append((t0, min(width, C + S - t0)))
    return tl


def build_program(S, C, L):
    T = C + S
    TT = T // 128
    nc = bass.Bass("TRN2", target_bir_lowering=False)

    def din(name, shape, dt=F32):
        return nc.dram_tensor(name, list(shape), dt, kind="ExternalInput")

    xin = din("xin", [D_MODEL, T])
    cvec = din("cvec", [128, 8, 2])
    w_mod = din("w_mod", [L, D_MODEL, 6144])
    b_mod = din("b_mod", [L, 128, 48])
    g1 = din("g1", [L, 128, 8])
    g2 = din("g2", [L, 128, 8])
    gfin = din("gfin", [128, 8])
    w_a = din("w_a", [L, D_MODEL, NA])
    w_g = din("w_g", [L, D_MODEL, 3072])
    b_gate = din("b_gate", [L, 128, 24])
    convp = din("convp", [L, 128, 4, 5])
    w_rgbd = din("w_rgbd", [L, 2, 4, 128, 128])
    w_igbd = din("w_igbd", [L, 2, 4, 128, 128])
    lruv = din("lruv", [L, 128, 2, 4, 3])
    qkg = din("qkg", [L, 2, 64])
    lamv = din("lamv", [L, 4, 64])
    subg = din("subg", [L, 128, 1])
    w_br = din("w_br", [L, 3, 512, D_MODEL])
    w_out = din("w_out", [L, D_MODEL, D_MODEL])
    w_up = din("w_up", [L, D_MODEL, 4096])
    w_down = din("w_down", [L, 4096, D_MODEL])
    rope = din("rope", [T, 64])
    outT = nc.dram_tensor("outT", [D_MODEL, S], F32, kind="ExternalOutput")

    def scr(name, shape, dt):
        return nc.dram_tensor(name, list(shape), dt)

    x1s = scr("x1s", [D_MODEL, T], F32)
    x2s = scr("x2s", [D_MODEL, T], F32)
    lxs = scr("lxs", [512, T], F32)
    lgs = scr("lgs", [512, T], F32)
    qgs = scr("qgs", [4, 128, T], BF16)
    kgs = scr("kgs", [2, 128, T], BF16)
    vgs = scr("vgs", [T, 256], BF16)
    dqs = scr("dqs", [4, 128, T], BF16)
    dks = scr("dks", [4, 128, T], BF16)
    dvs = scr("dvs", [T, 512], BF16)
    yrs = scr("yrs", [512, T], BF16)
    yas = scr("yas", [512, T], BF16)
    yds = scr("yds", [512, T], BF16)

    P = Prog(nc)
    top = ExitStack()

    ones_bf = sb(top, nc, "ones_bf", [128, 128], BF16)
    ones_f = sb(top, nc, "ones_f", [128, 128], F32)
    ident_f = sb(top, nc, "ident_f", [128, 128], F32)
    ident_bf = sb(top, nc, "ident_bf", [128, 128], BF16)
    scv = sb(top, nc, "scv", [128, 8, 2], F32)
    modv = sb(top, nc, "modv", [128, 48, 2], F32)
    G1 = sb(top, nc, "G1", [128, 8, 2], F32)
    G2 = sb(top, nc, "G2", [128, 8, 2], F32)
    GF = sb(top, nc, "GF", [128, 8], F32)
    g1t = sb(top, nc, "g1t", [128, 8], F32)
    g2t = sb(top, nc, "g2t", [128, 8], F32)
    bmt = sb(top, nc, "bmt", [128, 48], F32)
    bgt = sb(top, nc, "bgt", [128, 24], F32)
    cvt = sb(top, nc, "cvt", [128, 4, 5], F32)
    lvt = sb(top, nc, "lvt", [128, 2, 4, 3], F32)
    cch = sb(top, nc, "cch", [128, 2, 4], F32)
    qkgt = sb(top, nc, "qkgt", [128, 2, 64], F32)
    lamt = sb(top, nc, "lamt", [1, 4, 64], F32)
    lamw = sb(top, nc, "lamw", [1, 8], F32)
    neglam = sb(top, nc, "neglam", [128, 1], F32)
    gsub = sb(top, nc, "gsub", [128, 1], F32)
    b_const = Buf()
    b_vec = Buf()

    P.op("pool", lambda e: e.memset(ones_bf[:], 1.0), writes=[b_const])
    P.op("pool", lambda e: e.memset(ones_f[:], 1.0), writes=[b_const])
    P.op("pool", lambda e: e.memset(ident_f[:], 1.0), writes=[b_const])
    P.op("pool", lambda e: e.affine_select(out=ident_f[:], in_=ident_f[:], pattern=[[-1, 128]],
                                           compare_op=ALU.is_equal, fill=0.0, base=0, channel_multiplier=1),
         reads=[b_const], writes=[b_const])
    P.op("dve", lambda e: e.tensor_copy(out=ident_bf[:], in_=ident_f[:]), reads=[b_const], writes=[b_const])
    P.dma("sp", lambda e: e.dma_start(out=scv[:], in_=cvec.ap()), writes=[b_vec])
    P.dma("sp", lambda e: e.dma_start(out=GF[:], in_=gfin.ap()), writes=[b_vec])
    P.op("act", lambda e: e.activation(out=scv[:], in_=scv[:], func=AF.Silu), reads=[b_vec], writes=[b_vec])
    P.op("dve", lambda e: e.tensor_scalar(out=GF[:], in0=GF[:], scalar1=32.0, scalar2=None, op0=ALU.mult),
         reads=[b_vec], writes=[b_vec])
    P.flush()

    def norm_mod(xt, bx, n, Gt, SHt, col, hT, bh, sq_rot, ps_rot, rs_rot, tmp_rot):
        sq, bsq = sq_rot.next()
        half = 4
        for hh in range(2):
            P.op("act", lambda e, hh=hh: e.activation(out=sq[:, hh * half:(hh + 1) * half, :n],
                                                      in_=xt[:, hh * half:(hh + 1) * half, :n], func=AF.Square),
                 reads=bx, writes=[bsq[hh]])
        ps, bps = ps_rot.next()
        for c in range(8):
            P.op("pe", lambda e, c=c: e.matmul(ps[:, :n], lhsT=ones_bf[:], rhs=sq[:, c, :n], start=(c == 0), stop=(c == 7)),
                 reads=[bsq[c // half]], writes=[bps])
        rs, brs = rs_rot.next()
        P.op("act", lambda e: e.activation(out=rs[:, :n], in_=ps[:, :n], func=AF.Sqrt, scale=1.0, bias=1024.0 * EPS),
             reads=[bps], writes=[brs])
        P.op("dve", lambda e: e.reciprocal(out=rs[:, :n], in_=rs[:, :n]), reads=[brs], writes=[brs])
        for c in range(8):
            tmp, btmp = tmp_rot.next()
            P.op("dve", lambda e, c=c, tmp=tmp: e.scalar_tensor_tensor(out=tmp[:, :n], in0=xt[:, c, :n], scalar=Gt[:, c, col:col + 1],
                                                                    in1=rs[:, :n], op0=ALU.mult, op1=ALU.mult),
                 reads=bx + [brs, b_vec], writes=[btmp])
            P.op("act", lambda e, c=c, tmp=tmp: e.activation(out=hT[:, c, :n], in_=tmp[:, :n], func=AF.Identity,
                                                             bias=SHt[:, c, col:col + 1], scale=1.0),
                 reads=[btmp, b_vec], writes=[bh[c]])

    def load_w_bf16(wt, src_l, ncols, bw, nchunk=8):
        for c in range(nchunk):
            P.dma("pool", lambda e, c=c: e.dma_start(out=wt[:, c, :], in_=src_l[c * 128:(c + 1) * 128, :]), writes=[bw])

    for l in range(L):
        last = (l == L - 1)
        lam_init = 0.8 - 0.6 * math.exp(-0.3 * l)
        Xl = xin if l == 0 else x2s

        with ExitStack() as es:
            wm_rot = Rot(es, nc, "wm", [128, 8, 1024], F32, 2)
            pm = pstile(es, nc, "pm", [128, 16], F32)
            bpm = Buf()
            P.dma("sp", lambda e: e.dma_start(out=bmt[:], in_=b_mod.ap()[l]), writes=[b_vec])
            P.dma("sp", lambda e: e.dma_start(out=g1t[:], in_=g1.ap()[l]), writes=[b_vec])
            P.dma("sp", lambda e: e.dma_start(out=g2t[:], in_=g2.ap()[l]), writes=[b_vec])
            P.dma("sp", lambda e: e.dma_start(out=bgt[:], in_=b_gate.ap()[l]), writes=[b_vec])
            P.dma("sp", lambda e: e.dma_start(out=cvt[:], in_=convp.ap()[l]), writes=[b_vec])
            P.dma("sp", lambda e: e.dma_start(out=lvt[:], in_=lruv.ap()[l]), writes=[b_vec])
            P.dma("sp", lambda e: e.dma_start(out=gsub[:], in_=subg.ap()[l]), writes=[b_vec])
            P.dma("sp", lambda e: e.dma_start(out=lamt[:], in_=bass.AP(lamv, l * 256, [[256, 1], [64, 4], [1, 64]])), writes=[b_vec])
            P.dma("sp", lambda e: e.dma_start(out=qkgt[:], in_=bass.AP(qkg, l * 128, [[0, 128], [64, 2], [1, 64]])), writes=[b_vec])
            for grp in range(6):
                wm, bwm = wm_rot.next()
                P.dma("sp", lambda e, wm=wm, grp=grp: e.dma_start(
                    out=wm[:], in_=w_mod.ap()[l][:, grp * 1024:(grp + 1) * 1024].rearrange("(c p) n -> p c n", p=128)), writes=[bwm])
                for j in range(8):
                    for c in range(8):
                        P.op("pe", lambda e, wm=wm, j=j, c=c: e.matmul(pm[:, 2 * j:2 * j + 2], lhsT=wm[:, c, j * 128:(j + 1) * 128],
                                                                      rhs=scv[:, c, :], start=(c == 0), stop=(c == 7)),
                             reads=[bwm, b_vec], writes=[bpm])
                P.op("dve", lambda e, grp=grp: e.tensor_tensor(out=modv[:, grp * 8:(grp + 1) * 8, :], in0=pm[:].rearrange("p (j t) -> p j t", t=2),
                                                              in1=bass.AP(bmt, grp * 8, [[48, 128], [1, 8], [0, 2]]), op=ALU.add),
                     reads=[bpm, b_vec], writes=[b_vec])
            for (Gx, gx, sec) in ((G1, g1t, 1), (G2, g2t, 4)):
                P.op("dve", lambda e, Gx=Gx, sec=sec: e.tensor_scalar(out=Gx[:], in0=modv[:, sec * 8:(sec + 1) * 8, :], scalar1=1.0, scalar2=32.0,
                                                                      op0=ALU.add, op1=ALU.mult), reads=[b_vec], writes=[b_vec])
                P.op("dve", lambda e, Gx=Gx, gx=gx: e.tensor_tensor(out=Gx[:], in0=Gx[:], in1=bass.AP(gx, 0, [[8, 128], [1, 8], [0, 2]]), op=ALU.mult),
                     reads=[b_vec], writes=[b_vec])
            P.op("act", lambda e: e.activation(out=cch[:], in_=lvt[:, :, :, 2], func=AF.Exp, scale=-1.0), reads=[b_vec], writes=[b_vec])
            P.op("act", lambda e: e.activation(out=cch[:], in_=cch[:], func=AF.Ln, bias=1.0, scale=1.0), reads=[b_vec], writes=[b_vec])
            P.op("dve", lambda e: e.tensor_scalar(out=cch[:], in0=cch[:], scalar1=-8.0, scalar2=None, op0=ALU.mult), reads=[b_vec], writes=[b_vec])
            P.op("dve", lambda e: e.tensor_tensor(out=lamt[:, 0, :], in0=lamt[:, 0, :], in1=lamt[:, 1, :], op=ALU.mult), reads=[b_vec], writes=[b_vec])
            P.op("dve", lambda e: e.tensor_tensor(out=lamt[:, 2, :], in0=lamt[:, 2, :], in1=lamt[:, 3, :], op=ALU.mult), reads=[b_vec], writes=[b_vec])
            P.op("dve", lambda e: e.tensor_reduce(out=lamw[:, 0:1], in_=lamt[:, 0, :], axis=AX.X, op=ALU.add), reads=[b_vec], writes=[b_vec])
            P.op("dve", lambda e: e.tensor_reduce(out=lamw[:, 1:2], in_=lamt[:, 2, :], axis=AX.X, op=ALU.add), reads=[b_vec], writes=[b_vec])
            P.op("act", lambda e: e.activation(out=lamw[:, 2:4], in_=lamw[:, 0:2], func=AF.Exp), reads=[b_vec], writes=[b_vec])
            P.op("dve", lambda e: e.tensor_tensor(out=lamw[:, 4:5], in0=lamw[:, 3:4], in1=lamw[:, 2:3], op=ALU.subtract), reads=[b_vec], writes=[b_vec])
            P.op("dve", lambda e: e.tensor_scalar(out=lamw[:, 5:6], in0=lamw[:, 4:5], scalar1=-lam_init, scalar2=None, op0=ALU.add), reads=[b_vec], writes=[b_vec])
            P.op("pe", lambda e: e.matmul(pm[:, 0:1], lhsT=ones_f[0:1, :], rhs=lamw[0:1, 5:6], start=True, stop=True), reads=[b_vec, b_const], writes=[bpm])
            P.op("dve", lambda e: e.tensor_copy(out=neglam[:], in_=pm[:, 0:1]), reads=[bpm], writes=[b_vec])
            P.op("dve", lambda e: e.tensor_scalar(out=gsub[:], in0=gsub[:], scalar1=math.sqrt(128.0) * (1.0 - lam_init), scalar2=None, op0=ALU.mult),
                 reads=[b_vec], writes=[b_vec])
            P.flush()

        with ExitStack() as es:
            wa = sb(es, nc, "wa", [128, 8, NA], BF16)
            bwa = Buf()
            load_w_bf16(wa, w_a.ap()[l], NA, bwa)
            ropet = sb(es, nc, "ropet", [128, TT, 64], F32)
            brope = Buf()
            for j0 in range(0, TT, 8):
                j1 = min(TT, j0 + 8)
                P.dma("sp", lambda e, j0=j0, j1=j1: e.dma_start(out=ropet[:, j0:j1, :], in_=rope.ap()[j0 * 128:j1 * 128, :].rearrange("(j p) d -> p j d", p=128)), writes=[brope])
            x_rot = Rot(es, nc, "xa", [128, 8, 512], F32, 2)
            h_rot = Rot(es, nc, "ha", [128, 8, 512], BF16, 2)
            h_bufs = [bufs(8) for _ in range(2)]
            sq_rot = Rot(es, nc, "sqa", [128, 8, 512], BF16, 1)
            sq_bufs = [bufs(2)]
            rs_rot = Rot(es, nc, "rsa", [128, 512], F32, 1)
            tmp_rot = Rot(es, nc, "tma", [128, 512], F32, 2)
            ps_rot = Rot(es, nc, "psa", [128, 512], F32, 7, psum=True)
            pst = pstile(es, nc, "psta", [128, 8, 128], BF16)
            bpst = bufs(2)
            lx_rot = Rot(es, nc, "lxst", [128, 4, 512], F32, 1)
            lg_rot = Rot(es, nc, "lgst", [128, 4, 512], F32, 1)
            gl_rot = Rot(es, nc, "glt", [128, 4, 512], F32, 1)
            qT_rot = Rot(es, nc, "qTst", [128, 4, 512], BF16, 1)
            kT_rot = Rot(es, nc, "kTst", [128, 2, 512], BF16, 1)
            dqT_rot = Rot(es, nc, "dqTst", [128, 4, 512], BF16, 1)
            dkT_rot = Rot(es, nc, "dkTst", [128, 4, 512], BF16, 1)
            v_rot = Rot(es, nc, "vst", [128, 4, 2, 128], BF16, 1)
            dv_rot = Rot(es, nc, "dvst", [128, 4, 512], BF16, 1)
            zf_rot = Rot(es, nc, "zf", [128, 512], F32, 3)
            zt_rot = Rot(es, nc, "zt", [128, 4, 256], F32, 2)
            zb_rot = Rot(es, nc, "zb", [128, 512], BF16, 3)
            ssq = Rot(es, nc, "ssq", [128, 16], F32, 3)
            for vt in v_rot.tiles:
                P.op("pool", lambda e, vt=vt: e.memset(vt[:], 1.0), writes=v_rot.bufs)

            class _SqRot:
                def __init__(self):
                    self.r = sq_rot

                def next(self):
                    t, _ = self.r.next()
                    return t, sq_bufs[0]

            def rope_apply(zn, bzn, H, j, outb, bo):
                W = H * 64

                def v4(t, off, pstride):
                    return bass.AP(t, off, [[pstride, 128], [64, H], [32, 2], [1, 16]])

                def tab(off):
                    return bass.AP(ropet, j * 64 + off, [[TT * 64, 128], [0, H], [16, 2], [1, 16]])

                tt, btt = zt_rot.next()

                def tv(k):
                    return bass.AP(tt, k * 256, [[1024, 128], [32, H], [16, 2], [1, 16]])
                P.op("dve", lambda e: e.tensor_tensor(out=tv(0), in0=v4(zn, 0, 512), in1=tab(0), op=ALU.mult), reads=[bzn, brope], writes=[btt])
                P.op("pool", lambda e: e.tensor_tensor(out=tv(1), in0=v4(zn, 16, 512), in1=tab(32), op=ALU.mult), reads=[bzn, brope], writes=[btt])
                P.op("pool", lambda e: e.tensor_tensor(out=tv(2), in0=v4(zn, 0, 512), in1=tab(32), op=ALU.mult), reads=[bzn, brope], writes=[btt])
                P.op("dve", lambda e: e.tensor_tensor(out=tv(3), in0=v4(zn, 16, 512), in1=tab(0), op=ALU.mult), reads=[bzn, brope], writes=[btt])
                P.op("dve", lambda e: e.tensor_tensor(out=v4(outb, 0, 512), in0=tv(0), in1=tv(1), op=ALU.subtract), reads=[btt], writes=[bo])
                P.op("pool", lambda e: e.tensor_tensor(out=v4(outb, 16, 512), in0=tv(2), in1=tv(3), op=ALU.add), reads=[btt], writes=[bo])

            def qk_norm(ps, bps, off, H, gi, zn, bzn):
                W = H * 64
                sqt, bsqt = zf_rot.next()
                P.op("act", lambda e: e.activation(out=sqt[:, :W], in_=ps[:, off:off + W], func=AF.Square), reads=[bps], writes=[bsqt])
                s, bs = ssq.next()
                P.op("dve", lambda e: e.tensor_reduce(out=s[:, 0:H], in_=sqt[:, :W].rearrange("p (h d) -> p h d", d=64), axis=AX.X, op=ALU.add),
                     reads=[bsqt], writes=[bs])
                P.op("act", lambda e: e.activation(out=s[:, 0:H], in_=s[:, 0:H], func=AF.Sqrt, scale=1.0 / 64.0, bias=EPS), reads=[bs], writes=[bs])
                P.op("dve", lambda e: e.reciprocal(out=s[:, 0:H], in_=s[:, 0:H]), reads=[bs], writes=[bs])
                P.op("dve", lambda e: e.tensor_tensor(out=zn[:, :W].rearrange("p (h d) -> p h d", d=64),
                                                      in0=ps[:, off:off + W].rearrange("p (h d) -> p h d", d=64),
                                                      in1=bass.AP(s, 0, [[16, 128], [1, H], [0, 64]]), op=ALU.mult), reads=[bps, bs], writes=[bzn])
                P.op("pool", lambda e: e.tensor_tensor(out=zn[:, :W].rearrange("p (h d) -> p h d", d=64),
                                                       in0=zn[:, :W].rearrange("p (h d) -> p h d", d=64),
                                                       in1=bass.AP(qkgt, gi * 64, [[128, 128], [0, H], [1, 64]]), op=ALU.mult), reads=[bzn, b_vec], writes=[bzn])

            tiles = token_tiles(S, C, 512)
            pst_i = [0]
            for ti, (t0, n) in enumerate(tiles):
                col = 1 if t0 < C else 0
                nb = n // 128
                xt, bx = x_rot.next()
                P.dma("sp", lambda e, xt=xt, t0=t0, n=n: e.dma_start(out=xt[:, :, :n], in_=Xl.ap()[:, t0:t0 + n].rearrange("(c p) t -> p c t", p=128)), writes=[bx])
                hk = h_rot.i % 2
                hT, _ = h_rot.next()
                bh = h_bufs[hk]
                norm_mod(xt, [bx], n, G1, modv, col, hT, bh, _SqRot(), ps_rot, rs_rot, tmp_rot)
                lxt, blx = lx_rot.next()
                lgt, blg = lg_rot.next()
                for fc in range(8):
                    ps, bps = ps_rot.next()
                    for c in range(8):
                        P.op("pe", lambda e, ps=ps, fc=fc, c=c: e.matmul(ps[:, :n], lhsT=wa[:, c, OFF_LX + fc * 128:OFF_LX + (fc + 1) * 128], rhs=hT[:, c, :n],
                                                                        start=(c == 0), stop=(c == 7)), reads=[bwa, bh[c]], writes=[bps])
                    if fc < 4:
                        P.op("act", lambda e, ps=ps, fc=fc: e.activation(out=lxt[:, fc, :n], in_=ps[:, :n], func=AF.Identity, scale=1.0, bias=0.0), reads=[bps], writes=[blx])
                    else:
                        P.op("act", lambda e, ps=ps, fc=fc: e.activation(out=lgt[:, fc - 4, :n], in_=ps[:, :n], func=AF.Identity, scale=1.0, bias=0.0), reads=[bps], writes=[blg])
                glt, bgl = gl_rot.next()
                KG = 2.0 * math.sqrt(2.0 / math.pi)
                P.op("pool", lambda e: e.tensor_tensor(out=glt[:, :, :n], in0=lgt[:, :, :n], in1=lgt[:, :, :n], op=ALU.mult), reads=[blg], writes=[bgl])
                P.op("dve", lambda e: e.tensor_scalar(out=glt[:, :, :n], in0=glt[:, :, :n], scalar1=0.044715 * KG, scalar2=KG, op0=ALU.mult, op1=ALU.add), reads=[bgl], writes=[bgl])
                P.op("pool", lambda e: e.tensor_tensor(out=glt[:, :, :n], in0=glt[:, :, :n], in1=lgt[:, :, :n], op=ALU.mult), reads=[bgl, blg], writes=[bgl])
                P.op("act", lambda e: e.activation(out=glt[:, :, :n], in_=glt[:, :, :n], func=AF.Sigmoid), reads=[bgl], writes=[bgl])
                P.op("dve", lambda e: e.tensor_tensor(out=lgt[:, :, :n], in0=glt[:, :, :n], in1=lgt[:, :, :n], op=ALU.mult), reads=[bgl, blg], writes=[blg])
                P.dma("pool", lambda e, lxt=lxt, t0=t0, n=n: e.dma_start(out=lxs.ap()[:, t0:t0 + n].rearrange("(c p) t -> p c t", p=128), in_=lxt[:, :, :n]), reads=[blx])
                P.dma("pool", lambda e, lgt=lgt, t0=t0, n=n: e.dma_start(out=lgs.ap()[:, t0:t0 + n].rearrange("(c p) t -> p c t", p=128), in_=lgt[:, :, :n]), reads=[blg])
                qTt, bqT = qT_rot.next()
                kTt, bkT = kT_rot.next()
                dqTt, bdqT = dqT_rot.next()
                dkTt, bdkT = dkT_rot.next()
                vt, bv = v_rot.next()
                dvt, bdv = dv_rot.next()
                for tb in range(nb):
                    j = t0 // 128 + tb

                    def tok_mm(off, W):
                        ps, bps = ps_rot.next()
                        for c in range(8):
                            P.op("pe", lambda e, ps=ps, c=c: e.matmul(ps[:, :W], lhsT=hT[:, c, tb * 128:(tb + 1) * 128], rhs=wa[:, c, off:off + W],
                                                                     start=(c == 0), stop=(c == 7)), reads=[bwa, bh[c]], writes=[bps])
                        return ps, bps

                    def transp(src, bsrc, nch, dst, bdst):
                        k = pst_i[0] % 2
                        pst_i[0] += 1
                        for q in range(nch):
                            P.op("pe", lambda e, q=q: e.transpose(pst[:, k * 4 + q, :], src[:, q * 128:(q + 1) * 128], ident_bf[:]), reads=[bsrc, b_const], writes=[bpst[k]])
                        P.op("act", lambda e: e.copy(out=dst[:, 0:nch, tb * 128:(tb + 1) * 128], in_=pst[:, k * 4:k * 4 + nch, :]), reads=[bpst[k]], writes=[bdst])

                    ps, bps = tok_mm(OFF_GQ, 512)
                    zn, bzn = zf_rot.next()
                    qk_norm(ps, bps, 0, 8, 0, zn, bzn)
                    zb, bzb = zb_rot.next()
                    rope_apply(zn, bzn, 8, j, zb, bzb)
                    transp(zb, bzb, 4, qTt, bqT)
                    ps, bps = tok_mm(OFF_GK, 384)
                    zn, bzn = zf_rot.next()
                    qk_norm(ps, bps, 0, 4, 1, zn, bzn)
                    zb, bzb = zb_rot.next()
                    rope_apply(zn, bzn, 4, j, zb, bzb)
                    transp(zb, bzb, 2, kTt, bkT)
                    P.op("act", lambda e, ps=ps, vt=vt: e.copy(out=vt[:, tb, :, 0:64], in_=ps[:, 256:384].rearrange("p (g d) -> p g d", d=64)), reads=[bps], writes=[bv])
                    for (off, dstT, bdstT) in ((OFF_DQ, dqTt, bdqT), (OFF_DK, dkTt, bdkT)):
                        ps, bps = tok_mm(off, 512)
                        zn, bzn = zf_rot.next()
                        P.op("act", lambda e, ps=ps, zn=zn: e.copy(out=zn[:], in_=ps[:]), reads=[bps], writes=[bzn])
                        zb, bzb = zb_rot.next()
                        rope_apply(zn, bzn, 8, j, zb, bzb)
                        transp(zb, bzb, 4, dstT, bdstT)
                    ps, bps = tok_mm(OFF_DV, 512)
                    P.op("act", lambda e, ps=ps, dvt=dvt: e.copy(out=dvt[:, tb, :], in_=ps[:]), reads=[bps], writes=[bdv])
                P.dma("pool", lambda e, qTt=qTt, t0=t0, n=n: e.dma_start(out=qgs.ap()[:, :, t0:t0 + n].rearrange("h r t -> r h t"), in_=qTt[:, :, :n]), reads=[bqT])
                P.dma("pool", lambda e, kTt=kTt, t0=t0, n=n: e.dma_start(out=kgs.ap()[:, :, t0:t0 + n].rearrange("h r t -> r h t"), in_=kTt[:, :, :n]), reads=[bkT])
                P.dma("pool", lambda e, dqTt=dqTt, t0=t0, n=n: e.dma_start(out=dqs.ap()[:, :, t0:t0 + n].rearrange("h r t -> r h t"), in_=dqTt[:, :, :n]), reads=[bdqT])
                P.dma("pool", lambda e, dkTt=dkTt, t0=t0, n=n: e.dma_start(out=dks.ap()[:, :, t0:t0 + n].rearrange("h r t -> r h t"), in_=dkTt[:, :, :n]), reads=[bdkT])
                P.dma("pool", lambda e, vt=vt, t0=t0, n=n, nb=nb: e.dma_start(out=vgs.ap()[t0:t0 + n, :].rearrange("(b p) x -> p b x", p=128),
                                                                             in_=vt[:, 0:nb].rearrange("p b g d -> p b (g d)")), reads=[bv])
                P.dma("pool", lambda e, dvt=dvt, t0=t0, n=n, nb=nb: e.dma_start(out=dvs.ap()[t0:t0 + n, :].rearrange("(b p) x -> p b x", p=128), in_=dvt[:, 0:nb, :]), reads=[bdv])
            P.flush()

        with ExitStack() as es:
            wbd = sb(es, nc, "wbd", [128, 2, 2, 128], F32)
            bwbd = Buf()
            Xa = sb(es, nc, "Xa", [128, T], F32)
            Ua = sb(es, nc, "Ua", [128, T], F32)
            H0 = sb(es, nc, "H0", [128, T], F32)
            bX, bU, bH0 = Buf(), Buf(), Buf()
            ps_rot = Rot(es, nc, "psb", [128, 512], F32, 4, psum=True)
            r_rot = Rot(es, nc, "rb", [128, 512], F32, 2)
            i_rot = Rot(es, nc, "ib", [128, 512], F32, 2)
            a_rot = Rot(es, nc, "ab", [128, 512], F32, 2)
            s_rot = Rot(es, nc, "sb_", [128, 512], F32, 2)
            hb_rot = Rot(es, nc, "hb", [128, 512], F32, 3)
            g_rot = Rot(es, nc, "gb", [128, 512], F32, 2)
            y_rot = Rot(es, nc, "yb", [128, 512], BF16, 2)
            segs = [(0, C), (C, T)]
            tiles = token_tiles(S, C, 512)
            for ch in range(4):
                for d in range(2):
                    P.dma("sp", lambda e, d=d, ch=ch: e.dma_start(out=wbd[:, d, 0, :], in_=w_rgbd.ap()[l, d, ch]), writes=[bwbd])
                    P.dma("sp", lambda e, d=d, ch=ch: e.dma_start(out=wbd[:, d, 1, :], in_=w_igbd.ap()[l, d, ch]), writes=[bwbd])
                P.dma("sp", lambda e, ch=ch: e.dma_start(out=Xa[:], in_=lxs.ap()[ch * 128:(ch + 1) * 128, :]), writes=[bX])
                for (s0, s1) in segs:
                    for p0 in range(s0, s1, 2048):
                        p1 = min(p0 + 2048, s1)
                        P.op("dve", lambda e, p0=p0, p1=p1, ch=ch: e.tensor_scalar(out=Ua[:, p0:p1], in0=Xa[:, p0:p1], scalar1=cvt[:, ch, 2:3], scalar2=cvt[:, ch, 4:5],
                                                                                 op0=ALU.mult, op1=ALU.add), reads=[bX, b_vec], writes=[bU])
                    for p0 in range(s0, s1, 2048):
                        p1 = min(p0 + 2048, s1)
                        for (tap, sh) in ((0, -2), (1, -1), (3, 1)):
                            o0 = max(p0, s0 - sh) if sh < 0 else p0
                            o1 = p1 if sh < 0 else min(p1, s1 - sh)
                            P.op("dve", lambda e, o0=o0, o1=o1, sh=sh, tap=tap, ch=ch: e.scalar_tensor_tensor(
                                out=Ua[:, o0:o1], in0=Xa[:, o0 + sh:o1 + sh], scalar=cvt[:, ch, tap:tap + 1], in1=Ua[:, o0:o1], op0=ALU.mult, op1=ALU.add),
                                reads=[bX, bU, b_vec], writes=[bU])

                def coeffs(d, t0, n):
                    psr, bpsr = ps_rot.next()
                    psi, bpsi = ps_rot.next()
                    P.op("pe", lambda e: e.matmul(psr[:, :n], lhsT=wbd[:, d, 0, :], rhs=Ua[:, t0:t0 + n], start=True, stop=True), reads=[bwbd, bU], writes=[bpsr])
                    P.op("pe", lambda e: e.matmul(psi[:, :n], lhsT=wbd[:, d, 1, :], rhs=Ua[:, t0:t0 + n], start=True, stop=True), reads=[bwbd, bU], writes=[bpsi])
                    r, br = r_rot.next()
                    ii, bi = i_rot.next()
                    P.op("act", lambda e: e.activation(out=r[:, :n], in_=psr[:, :n], func=AF.Sigmoid, bias=lvt[:, d, ch, 0:1], scale=1.0), reads=[bpsr, b_vec], writes=[br])
                    P.op("act", lambda e: e.activation(out=ii[:, :n], in_=psi[:, :n], func=AF.Sigmoid, bias=lvt[:, d, ch, 1:2], scale=1.0), reads=[bpsi, b_vec], writes=[bi])
                    a, ba = a_rot.next()
                    P.op("act", lambda e: e.activation(out=a[:, :n], in_=r[:, :n], func=AF.Exp, scale=cch[:, d, ch:ch + 1]), reads=[br, b_vec], writes=[ba])
                    s, bs = s_rot.next()
                    P.op("pool", lambda e: e.tensor_tensor(out=s[:, :n], in0=a[:, :n], in1=a[:, :n], op=ALU.mult), reads=[ba], writes=[bs])
                    P.op("act", lambda e: e.activation(out=s[:, :n], in_=s[:, :n], func=AF.Sqrt, scale=-1.0, bias=1.0), reads=[bs], writes=[bs])
                    P.op("dve", lambda e: e.tensor_tensor(out=ii[:, :n], in0=ii[:, :n], in1=Ua[:, t0:t0 + n], op=ALU.mult), reads=[bi, bU], writes=[bi])
                    P.op("dve", lambda e: e.tensor_tensor(out=ii[:, :n], in0=ii[:, :n], in1=s[:, :n], op=ALU.mult), reads=[bi, bs], writes=[bi])
                    return a, ba, ii, bi

                for ti, (t0, n) in enumerate(tiles):
                    a, ba, b_, bb = coeffs(0, t0, n)
                    init = 0.0 if ti == 0 else H0[:, t0 - 1:t0]
                    P.op("dve", lambda e, a=a, b_=b_, t0=t0, n=n, init=init: e.tensor_tensor_scan(out=H0[:, t0:t0 + n], data0=a[:, :n], data1=b_[:, :n], initial=init,
                                                                                                  op0=ALU.mult, op1=ALU.add), reads=[ba, bb, bH0], writes=[bH0])
                ctx_tiles = [tt_ for tt_ in tiles if tt_[0] < C]
                lat_tiles = [tt_ for tt_ in tiles if tt_[0] >= C]
                order = list(reversed(ctx_tiles)) + list(reversed(lat_tiles))
                prev = None
                for (t0, n) in order:
                    a, ba, b_, bb = coeffs(1, t0, n)
                    hb, bhb = hb_rot.next()
                    init = 0.0 if prev is None else prev[0][:, 0:1]
                    rd = ([prev[1]] if prev is not None else [])

                    def rev(t, n=n):
                        return bass.AP(t, n - 1, [[512, 128], [-1, n]])
                    P.op("dve", lambda e, a=a, b_=b_, hb=hb, init=init, rev=rev: e.tensor_tensor_scan(out=rev(hb), data0=rev(a), data1=rev(b_), initial=init,
                                                                                                   op0=ALU.mult, op1=ALU.add), reads=[ba, bb] + rd, writes=[bhb])
                    prev = (hb, bhb)
                    g, bg = g_rot.next()
                    P.dma("sp", lambda e, g=g, t0=t0, n=n, ch=ch: e.dma_start(out=g[:, :n], in_=lgs.ap()[ch * 128:(ch + 1) * 128, t0:t0 + n]), writes=[bg])
                    y, by = y_rot.next()
                    t2, bt2 = s_rot.next()
                    P.op("pool", lambda e, t2=t2, hb=hb, t0=t0, n=n: e.tensor_tensor(out=t2[:, :n], in0=hb[:, :n], in1=H0[:, t0:t0 + n], op=ALU.add), reads=[bhb, bH0], writes=[bt2])
                    P.op("dve", lambda e, y=y, t2=t2, g=g, n=n: e.tensor_tensor(out=y[:, :n], in0=t2[:, :n], in1=g[:, :n], op=ALU.mult), reads=[bt2, bg], writes=[by])
                    P.dma("pool", lambda e, y=y, t0=t0, n=n, ch=ch: e.dma_start(out=yrs.ap()[ch * 128:(ch + 1) * 128, t0:t0 + n], in_=y[:, :n]), reads=[by])
            P.flush()

        qtiles = token_tiles(S, C, 512, with_ctx=not last)

        def run_pipeline(groups):
            deferred = []
            ng = len(groups)
            if ng == 0:
                return
            groups[0]["pre"]()
            groups[0]["S"]()
            for i in range(ng):
                if i + 1 < ng:
                    groups[i + 1]["pre"]()
                    groups[i + 1]["S"]()
                groups[i]["exp"]()
                groups[i]["PV"]()
                for (dly, fn) in groups[i]["post"]():
                    deferred.append((i + dly, fn))
                ready = [d for d in deferred if d[0] <= i]
                deferred[:] = [d for d in deferred if d[0] > i]
                for d in ready:
                    d[1]()
            while deferred:
                deferred.pop(0)[1]()

        def exp_op(Sp, bS, Pt, bP, n):
            if n == 512:
                P.op("act", lambda e: e.activation(out=Pt[:], in_=Sp[:], func=AF.Exp, scale=0.125), reads=[bS], writes=[bP])
            else:
                P.op("act", lambda e: e.activation(out=Pt[:].rearrange("p (a q) -> p a q", a=2)[:, :, 0:n],
                                                   in_=Sp[:].rearrange("p (a q) -> p a q", a=2)[:, :, 0:n], func=AF.Exp, scale=0.125), reads=[bS], writes=[bP])

        with ExitStack() as es:
            psS = Rot(es, nc, "psS", [128, 1024], F32, 2, psum=True)
            psO = Rot(es, nc, "psO", [128, 512], F32, 4, psum=True)
            kTs = sb(es, nc, "kTs", [128, 2, T], BF16)
            vA = sb(es, nc, "vA", [128, TT, 256], BF16)
            bk, bvA = Buf(), Buf()
            for g in range(2):
                P.dma("sp", lambda e, g=g: e.dma_start(out=kTs[:, g, :], in_=kgs.ap()[g]), writes=[bk])
            for j0 in range(0, TT, 8):
                j1 = min(TT, j0 + 8)
                P.dma("sp", lambda e, j0=j0, j1=j1: e.dma_start(out=vA[:, j0:j1, :], in_=vgs.ap()[j0 * 128:j1 * 128, :].rearrange("(j p) x -> p j x", p=128)), writes=[bvA])
            q_rot = Rot(es, nc, "qc", [128, 512], BF16, 3)
            p_rot = Rot(es, nc, "pc", [128, 1024], BF16, 3)
            o_rot = Rot(es, nc, "oc", [128, 512], F32, 4)
            z_rot = Rot(es, nc, "zc", [64, 512], F32, 4)
            y_rot = Rot(es, nc, "yc", [64, 512], BF16, 4)
            groups = []

            def make_head_c1(t0, n, p):
                g = p // 2
                nkt = (C // 128) if t0 < C else TT
                st = {}

                def pre0():
                    st["qT"], st["bq"] = q_rot.next()
                    qT = st["qT"]
                    P.dma("sp", lambda e: e.dma_start(out=qT[:, :n], in_=qgs.ap()[p, :, t0:t0 + n]), writes=[st["bq"]])
                    st["OA"], st["bOA"] = psO.next()
                    st["OB"], st["bOB"] = psO.next()

                def mk(kt):
                    gs = {}

                    def pre():
                        if kt == 0:
                            pre0()

                    def S_():
                        Sp, bS = psS.next()
                        gs["Sp"], gs["bS"] = Sp, bS
                        qT, bq = st["qT"], st["bq"]
                        P.op("pe", lambda e: e.matmul(Sp[:, 0:n], lhsT=kTs[0:64, g, kt * 128:(kt + 1) * 128], rhs=qT[0:64, :n], start=True, stop=True), reads=[bk, bq], writes=[bS])
                        P.op("pe", lambda e: e.matmul(Sp[:, 512:512 + n], lhsT=kTs[64:128, g, kt * 128:(kt + 1) * 128], rhs=qT[64:128, :n], start=True, stop=True), reads=[bk, bq], writes=[bS])

                    def exp_():
                        Pt, bP = p_rot.next()
                        gs["Pt"], gs["bP"] = Pt, bP
                        exp_op(gs["Sp"], gs["bS"], Pt, bP, n)

                    def PV_():
                        Pt, bP = gs["Pt"], gs["bP"]
                        OA, OB = st["OA"], st["OB"]
                        P.op("pe", lambda e: e.matmul(OA[:, :n], lhsT=vA[:, kt, g * 128:(g + 1) * 128], rhs=Pt[:, 0:n], start=(kt == 0), stop=(kt == nkt - 1)), reads=[bvA, bP], writes=[st["bOA"]])
                        P.op("pe", lambda e: e.matmul(OB[:, :n], lhsT=vA[:, kt, g * 128:(g + 1) * 128], rhs=Pt[:, 512:512 + n], start=(kt == 0), stop=(kt == nkt - 1)), reads=[bvA, bP], writes=[st["bOB"]])

                    def post():
                        if kt != nkt - 1:
                            return []
                        for hi, (Ox, bOx) in enumerate(((st["OA"], st["bOA"]), (st["OB"], st["bOB"]))):
                            h = 2 * p + hi
                            osb, bosb = o_rot.next()
                            P.op("dve", lambda e, osb=osb, Ox=Ox: e.tensor_copy(out=osb[:, :n], in_=Ox[:, :n]), reads=[bOx], writes=[bosb])
                            zt, bzt = z_rot.next()
                            P.dma("sp", lambda e, zt=zt, osb=osb: e.dma_start(out=zt[:, :n], in_=osb[64:128, :n]), reads=[bosb], writes=[bzt])
                            P.op("dve", lambda e, zt=zt: e.reciprocal(out=zt[:, :n], in_=zt[:, :n]), reads=[bzt], writes=[bzt])
                            yt, byt = y_rot.next()
                            P.op("dve", lambda e, yt=yt, osb=osb, zt=zt: e.tensor_tensor(out=yt[:, :n], in0=osb[0:64, :n], in1=zt[:, :n], op=ALU.mult), reads=[bosb, bzt], writes=[byt])
                            P.dma("pool", lambda e, yt=yt, h=h: e.dma_start(out=yas.ap()[h * 64:(h + 1) * 64, t0:t0 + n], in_=yt[:, :n]), reads=[byt])
                        return []
                    return {"pre": pre, "S": S_, "exp": exp_, "PV": PV_, "post": post}
                for kt in range(nkt):
                    groups.append(mk(kt))

            for (t0, n) in qtiles:
                for p in range(4):
                    make_head_c1(t0, n, p)
            run_pipeline(groups)
            P.flush()

        with ExitStack() as es:
            psS = Rot(es, nc, "psS2", [128, 1024], F32, 2, psum=True)
            psO = Rot(es, nc, "psO2", [128, 512], F32, 4, psum=True)
            dkT = sb(es, nc, "dkT", [128, 4, T], BF16)
            dvA = sb(es, nc, "dvA", [128, TT, 512], BF16)
            bk, bvA = Buf(), Buf()
            for h in range(4):
                P.dma("sp", lambda e, h=h: e.dma_start(out=dkT[:, h, :], in_=dks.ap()[h]), writes=[bk])
            for j0 in range(0, TT, 8):
                j1 = min(TT, j0 + 8)
                P.dma("sp", lambda e, j0=j0, j1=j1: e.dma_start(out=dvA[:, j0:j1, :], in_=dvs.ap()[j0 * 128:j1 * 128, :].rearrange("(j p) x -> p j x", p=128)), writes=[bvA])
            q_rot = Rot(es, nc, "qd", [128, 512], BF16, 3)
            p_rot = Rot(es, nc, "pd", [128, 1024], BF16, 3)
            r_rot = Rot(es, nc, "rd", [128, 512], F32, 4)
            o_rot = Rot(es, nc, "od", [128, 512], F32, 2)
            t_rot = Rot(es, nc, "td", [128, 512], F32, 2)
            sq_rot = Rot(es, nc, "sqd", [128, 512], BF16, 2)
            y_rot = Rot(es, nc, "yd", [128, 512], BF16, 2)
            groups = []

            acc_rot = Rot(es, nc, "accd", [128, 1024], F32, 4)
            acc_pb = [[Buf(), Buf()] for _ in range(4)]

            def make_head_c2(t0, n, h):
                nkt = (C // 128) if t0 < C else TT
                st = {}

                def pre0():
                    st["qT"], st["bq"] = q_rot.next()
                    qT = st["qT"]
                    P.dma("sp", lambda e: e.dma_start(out=qT[:, :n], in_=dqs.ap()[h, :, t0:t0 + n]), writes=[st["bq"]])
                    for nm in ("O1", "O2"):
                        st[nm], st["b" + nm] = psO.next()
                    st["acc"], st["accb"] = [], []
                    for _k in range(2):
                        st["accb"].append(acc_pb[acc_rot.i % 4])
                        st["acc"].append(acc_rot.next())

                def mk(kt):
                    gs = {}

                    def pre():
                        if kt == 0:
                            pre0()

                    def S_():
                        Sp, bS = psS.next()
                        gs["Sp"], gs["bS"] = Sp, bS
                        qT, bq = st["qT"], st["bq"]
                        P.op("pe", lambda e: e.matmul(Sp[:, 0:n], lhsT=dkT[0:64, h, kt * 128:(kt + 1) * 128], rhs=qT[0:64, :n], start=True, stop=True), reads=[bk, bq], writes=[bS])
                        P.op("pe", lambda e: e.matmul(Sp[:, 512:512 + n], lhsT=dkT[64:128, h, kt * 128:(kt + 1) * 128], rhs=qT[64:128, :n], start=True, stop=True), reads=[bk, bq], writes=[bS])

                    def exp_():
                        Pt, bP = p_rot.next()
                        gs["Pt"], gs["bP"] = Pt, bP
                        exp_op(gs["Sp"], gs["bS"], Pt, bP, n)

                    def PV_():
                        Pt, bP = gs["Pt"], gs["bP"]
                        s0, s1 = (kt == 0), (kt == nkt - 1)
                        O1, O2 = st["O1"], st["O2"]
                        P.op("pe", lambda e: e.matmul(O1[:, :n], lhsT=dvA[:, kt, h * 128:(h + 1) * 128], rhs=Pt[:, 0:n], start=s0, stop=s1), reads=[bvA, bP], writes=[st["bO1"]])
                        P.op("pe", lambda e: e.matmul(O2[:, :n], lhsT=dvA[:, kt, h * 128:(h + 1) * 128], rhs=Pt[:, 512:512 + n], start=s0, stop=s1), reads=[bvA, bP], writes=[st["bO2"]])
                        (acc, _), ab = st["acc"][kt % 2], st["accb"][kt % 2]
                        parts = (("dve", 0, 1024, ab[0]),)
                        for (eng, c0, c1, bb) in parts:
                            if kt < 2:
                                P.op(eng, lambda e: e.tensor_copy(out=acc[:, c0:c1], in_=Pt[:, c0:c1]), reads=[bP], writes=[bb])
                            else:
                                P.op(eng, lambda e: e.tensor_tensor(out=acc[:, c0:c1], in0=acc[:, c0:c1], in1=Pt[:, c0:c1], op=ALU.add), reads=[bP, bb], writes=[bb])

                    def post():
                        if kt != nkt - 1:
                            return []
                        O1, O2 = st["O1"], st["O2"]
                        bO1, bO2 = st["bO1"], st["bO2"]
                        hold = {}

                        def stage1():
                            Zx, bZx = psS.next()
                            psS.i += 1
                            for half in range(2):
                                for k in range(2):
                                    acc = st["acc"][k][0]
                                    P.op("pe", lambda e, acc=acc, half=half, k=k: e.matmul(Zx[:, half * 512:half * 512 + n], lhsT=ones_f[:], rhs=acc[:, half * 512:half * 512 + n],
                                                                                        start=(k == 0), stop=(k == 1)), reads=[b_const] + st["accb"][k], writes=[bZx])
                            r1, br1 = r_rot.next()
                            r2, br2 = r_rot.next()
                            P.op("dve", lambda e: e.reciprocal(out=r1[:, :n], in_=Zx[:, 0:n]), reads=[bZx], writes=[br1])
                            P.op("dve", lambda e: e.reciprocal(out=r2[:, :n], in_=Zx[:, 512:512 + n]), reads=[bZx], writes=[br2])
                            o, bo = o_rot.next()
                            tt_, btt = t_rot.next()
                            P.op("dve", lambda e: e.tensor_tensor(out=o[:, :n], in0=O1[:, :n], in1=r1[:, :n], op=ALU.mult), reads=[bO1, br1], writes=[bo])
                            P.op("dve", lambda e: e.tensor_tensor(out=tt_[:, :n], in0=O2[:, :n], in1=r2[:, :n], op=ALU.mult), reads=[bO2, br2], writes=[btt])
                            P.op("dve", lambda e: e.scalar_tensor_tensor(out=o[:, :n], in0=tt_[:, :n], scalar=neglam[:, 0:1], in1=o[:, :n], op0=ALU.mult, op1=ALU.add),
                                 reads=[bo, btt, b_vec], writes=[bo])
                            sq, bsq = sq_rot.next()
                            P.op("pool", lambda e: e.tensor_tensor(out=sq[:, :n], in0=o[:, :n], in1=o[:, :n], op=ALU.mult), reads=[bo], writes=[bsq])
                            hold.update(o=o, bo=bo, sq=sq, bsq=bsq)

                        def stage2():
                            o, bo, sq, bsq = hold["o"], hold["bo"], hold["sq"], hold["bsq"]
                            Sx, bSx = psS.next()
                            psS.i += 1
                            P.op("pe", lambda e: e.matmul(Sx[:, :n], lhsT=ones_bf[:], rhs=sq[:, :n], start=True, stop=True), reads=[b_const, bsq], writes=[bSx])
                            r3, br3 = r_rot.next()
                            P.op("act", lambda e: e.activation(out=r3[:, :n], in_=Sx[:, :n], func=AF.Ln, bias=128.0 * EPS, scale=1.0), reads=[bSx], writes=[br3])
                            P.op("act", lambda e: e.activation(out=r3[:, :n], in_=r3[:, :n], func=AF.Exp, scale=-0.5), reads=[br3], writes=[br3])
                            yt, byt = y_rot.next()
                            P.op("dve", lambda e: e.scalar_tensor_tensor(out=yt[:, :n], in0=o[:, :n], scalar=gsub[:, 0:1], in1=r3[:, :n], op0=ALU.mult, op1=ALU.mult),
                                 reads=[bo, br3, b_vec], writes=[byt])
                            P.dma("pool", lambda e: e.dma_start(out=yds.ap()[h * 128:(h + 1) * 128, t0:t0 + n], in_=yt[:, :n]), reads=[byt])
                        return [(2, stage1), (6, stage2)]
                    return {"pre": pre, "S": S_, "exp": exp_, "PV": PV_, "post": post}
                for kt in range(nkt):
                    groups.append(mk(kt))

            for (t0, n) in qtiles:
                for h in range(4):
                    make_head_c2(t0, n, h)
            run_pipeline(groups)
            P.flush()

        dtiles = token_tiles(S, C, 512, with_ctx=not last)
        with ExitStack() as es:
            wg = sb(es, nc, "wg", [128, 8, 3072], BF16)
            wb = sb(es, nc, "wb", [128, 12, 1024], BF16)
            wo = sb(es, nc, "wo", [128, 8, 1024], BF16)
            bwg, bwb, bwo = Buf(), Buf(), Buf()
            load_w_bf16(wg, w_g.ap()[l], 3072, bwg)
            load_w_bf16(wb, w_br.ap()[l].rearrange("n k m -> (n k) m"), 1024, bwb, nchunk=12)
            load_w_bf16(wo, w_out.ap()[l], 1024, bwo)
            x_rot = Rot(es, nc, "xd", [128, 8, 512], F32, 2)
            h_rot = Rot(es, nc, "hd", [128, 8, 512], BF16, 1)
            h_bufs = [bufs(8)]
            sq_rot = Rot(es, nc, "sqd1", [128, 8, 512], BF16, 1)
            sq_bufs = [bufs(2)]
            rs_rot = Rot(es, nc, "rsd", [128, 512], F32, 1)
            tmp_rot = Rot(es, nc, "tmd", [128, 512], F32, 2)
            ps_rot = Rot(es, nc, "psd", [128, 512], F32, 8, psum=True)
            y_rot = Rot(es, nc, "yd1", [128, 12, 512], BF16, 2)
            gt_rot = Rot(es, nc, "gtd", [128, 512], F32, 3)
            mt_rot = Rot(es, nc, "mtd", [128, 512], F32, 3)
            macc = Rot(es, nc, "macc", [128, 512], F32, 2)
            mbf = sb(es, nc, "mbf", [128, 8, 512], BF16)
            bmbf = bufs(8)

            class _SqRot1:
                def next(self):
                    t, _ = sq_rot.next()
                    return t, sq_bufs[0]

            for (t0, n) in dtiles:
                col = 1 if t0 < C else 0
                xt, bx = x_rot.next()
                P.dma("sp", lambda e, xt=xt, t0=t0, n=n: e.dma_start(out=xt[:, :, :n], in_=Xl.ap()[:, t0:t0 + n].rearrange("(c p) t -> p c t", p=128)), writes=[bx])
                yt, by = y_rot.next()
                for bi_, ysrc in enumerate((yrs, yas, yds)):
                    P.dma("sp", lambda e, yt=yt, bi_=bi_, ysrc=ysrc, t0=t0, n=n: e.dma_start(out=yt[:, bi_ * 4:(bi_ + 1) * 4, :n],
                                                                                           in_=ysrc.ap()[:, t0:t0 + n].rearrange("(c p) t -> p c t", p=128)), writes=[by])
                hT, _ = h_rot.next()
                bh = h_bufs[0]
                norm_mod(xt, [bx], n, G1, modv, col, hT, bh, _SqRot1(), ps_rot, rs_rot, tmp_rot)
                for fc in range(8):
                    acc, bacc = macc.next()
                    for nb_ in range(3):
                        psg, bpsg = ps_rot.next()
                        for c in range(8):
                            P.op("pe", lambda e, psg=psg, c=c, nb_=nb_, fc=fc: e.matmul(psg[:, :n], lhsT=wg[:, c, nb_ * 1024 + fc * 128:nb_ * 1024 + (fc + 1) * 128], rhs=hT[:, c, :n],
                                                                                      start=(c == 0), stop=(c == 7)), reads=[bwg, bh[c]], writes=[bpsg])
                        gt, bgt_ = gt_rot.next()
                        P.op("act", lambda e, gt=gt, psg=psg, nb_=nb_, fc=fc: e.activation(out=gt[:, :n], in_=psg[:, :n], func=AF.Sigmoid, bias=bgt[:, nb_ * 8 + fc:nb_ * 8 + fc + 1], scale=1.0),
                             reads=[bpsg, b_vec], writes=[bgt_])
                        psp, bpsp = ps_rot.next()
                        for kc in range(4):
                            P.op("pe", lambda e, psp=psp, kc=kc, nb_=nb_, fc=fc: e.matmul(psp[:, :n], lhsT=wb[:, nb_ * 4 + kc, fc * 128:(fc + 1) * 128], rhs=yt[:, nb_ * 4 + kc, :n],
                                                                                        start=(kc == 0), stop=(kc == 3)), reads=[bwb, by], writes=[bpsp])
                        if nb_ == 0:
                            P.op("dve", lambda e, acc=acc, psp=psp, gt=gt: e.tensor_tensor(out=acc[:, :n], in0=psp[:, :n], in1=gt[:, :n], op=ALU.mult), reads=[bpsp, bgt_], writes=[bacc])
                        else:
                            mt, bmt_ = mt_rot.next()
                            P.op("dve", lambda e, mt=mt, psp=psp, gt=gt: e.tensor_tensor(out=mt[:, :n], in0=psp[:, :n], in1=gt[:, :n], op=ALU.mult), reads=[bpsp, bgt_], writes=[bmt_])
                            if nb_ == 1:
                                P.op("pool", lambda e, acc=acc, mt=mt: e.tensor_tensor(out=acc[:, :n], in0=acc[:, :n], in1=mt[:, :n], op=ALU.add), reads=[bacc, bmt_], writes=[bacc])
                            else:
                                P.op("pool", lambda e, acc=acc, mt=mt, fc=fc: e.tensor_tensor(out=mbf[:, fc, :n], in0=acc[:, :n], in1=mt[:, :n], op=ALU.add), reads=[bacc, bmt_], writes=[bmbf[fc]])
                for oc in range(8):
                    pso, bpso = ps_rot.next()
                    for fc in range(8):
                        P.op("pe", lambda e, pso=pso, fc=fc, oc=oc: e.matmul(pso[:, :n], lhsT=wo[:, fc, oc * 128:(oc + 1) * 128], rhs=mbf[:, fc, :n], start=(fc == 0), stop=(fc == 7)),
                             reads=[bwo, bmbf[fc]], writes=[bpso])
                    P.op("dve", lambda e, pso=pso, oc=oc, xt=xt, col=col: e.scalar_tensor_tensor(out=xt[:, oc, :n], in0=pso[:, :n], scalar=modv[:, 16 + oc, col:col + 1], in1=xt[:, oc, :n],
                                                                                               op0=ALU.mult, op1=ALU.add), reads=[bpso, bx, b_vec], writes=[bx])
                P.dma("pool", lambda e, xt=xt, t0=t0, n=n: e.dma_start(out=x1s.ap()[:, t0:t0 + n].rearrange("(c p) t -> p c t", p=128), in_=xt[:, :, :n]), reads=[bx])
            P.flush()

        mtiles = token_tiles(S, C, 256, with_ctx=not last)
        with ExitStack() as es:
            wu = sb(es, nc, "wu", [128, 8, 4096], BF16)
            wd = sb(es, nc, "wd", [128, 32, 1024], BF16)
            bwu, bwd = Buf(), Buf()
            load_w_bf16(wu, w_up.ap()[l], 4096, bwu)
            load_w_bf16(wd, w_down.ap()[l], 1024, bwd, nchunk=32)
            x_rot = Rot(es, nc, "xm", [128, 8, 256], F32, 2)
            h_rot = Rot(es, nc, "hm", [128, 8, 256], BF16, 1)
            h_bufs = [bufs(8)]
            sq_rot = Rot(es, nc, "sqm", [128, 8, 256], BF16, 1)
            sq_bufs = [bufs(2)]
            rs_rot = Rot(es, nc, "rsm", [128, 256], F32, 1)
            tmp_rot = Rot(es, nc, "tmm", [128, 256], F32, 2)
            ps_rot = Rot(es, nc, "psm", [128, 512], F32, 8, psum=True)
            at = sb(es, nc, "atm", [128, 32, 256], BF16)
            bat = bufs(32)
            rl_rot = Rot(es, nc, "rlm", [128, 256], F32, 3)
            o_rot = Rot(es, nc, "om", [128, 8, 256], F32, 1)

            class _SqRot2:
                def next(self):
                    t, _ = sq_rot.next()
                    return t, sq_bufs[0]

            for (t0, n) in mtiles:
                col = 1 if t0 < C else 0
                xt, bx = x_rot.next()
                P.dma("sp", lambda e, xt=xt, t0=t0, n=n: e.dma_start(out=xt[:, :, :n], in_=x1s.ap()[:, t0:t0 + n].rearrange("(c p) t -> p c t", p=128)), writes=[bx])
                hT, _ = h_rot.next()
                bh = h_bufs[0]
                norm_mod(xt, [bx], n, G2, modv[:, 24:32, :], col, hT, bh, _SqRot2(), ps_rot, rs_rot, tmp_rot)
                for fc in range(32):
                    psu, bpsu = ps_rot.next()
                    for c in range(8):
                        P.op("pe", lambda e, psu=psu, c=c, fc=fc: e.matmul(psu[:, :n], lhsT=wu[:, c, fc * 128:(fc + 1) * 128], rhs=hT[:, c, :n], start=(c == 0), stop=(c == 7)),
                             reads=[bwu, bh[c]], writes=[bpsu])
                    rl, brl = rl_rot.next()
                    P.op("act", lambda e, rl=rl, psu=psu: e.activation(out=rl[:, :n], in_=psu[:, :n], func=AF.Relu), reads=[bpsu], writes=[brl])
                    P.op("dve" if fc % 2 == 0 else "pool", lambda e, rl=rl, fc=fc: e.tensor_tensor(out=at[:, fc, :n], in0=rl[:, :n], in1=rl[:, :n], op=ALU.mult), reads=[brl], writes=[bat[fc]])
                for oc in range(8):
                    psd, bpsd = ps_rot.next()
                    for fc in range(32):
                        P.op("pe", lambda e, psd=psd, fc=fc, oc=oc: e.matmul(psd[:, :n], lhsT=wd[:, fc, oc * 128:(oc + 1) * 128], rhs=at[:, fc, :n], start=(fc == 0), stop=(fc == 31)),
                             reads=[bwd, bat[fc]], writes=[bpsd])
                    P.op("dve", lambda e, psd=psd, oc=oc, xt=xt, col=col: e.scalar_tensor_tensor(out=xt[:, oc, :n], in0=psd[:, :n], scalar=modv[:, 40 + oc, col:col + 1], in1=xt[:, oc, :n],
                                                                                               op0=ALU.mult, op1=ALU.add), reads=[bpsd, bx, b_vec], writes=[bx])
                if not last:
                    P.dma("pool", lambda e, xt=xt, t0=t0, n=n: e.dma_start(out=x2s.ap()[:, t0:t0 + n].rearrange("(c p) t -> p c t", p=128), in_=xt[:, :, :n]), reads=[bx])
                else:
                    sq, bsq = _SqRot2().next()
                    for hh in range(2):
                        P.op("act", lambda e, sq=sq, xt=xt, hh=hh: e.activation(out=sq[:, hh * 4:(hh + 1) * 4, :n], in_=xt[:, hh * 4:(hh + 1) * 4, :n], func=AF.Square), reads=[bx], writes=[bsq[hh]])
                    ps, bps = ps_rot.next()
                    for c in range(8):
                        P.op("pe", lambda e, ps=ps, sq=sq, c=c: e.matmul(ps[:, :n], lhsT=ones_bf[:], rhs=sq[:, c, :n], start=(c == 0), stop=(c == 7)), reads=[bsq[c // 4], b_const], writes=[bps])
                    rs, brs = rs_rot.next()
                    P.op("act", lambda e, rs=rs, ps=ps: e.activation(out=rs[:, :n], in_=ps[:, :n], func=AF.Sqrt, scale=1.0, bias=1024.0 * EPS), reads=[bps], writes=[brs])
                    P.op("dve", lambda e, rs=rs: e.reciprocal(out=rs[:, :n], in_=rs[:, :n]), reads=[brs], writes=[brs])
                    ot, bot = o_rot.next()
                    for c in range(8):
                        P.op("dve", lambda e, ot=ot, xt=xt, rs=rs, c=c: e.scalar_tensor_tensor(out=ot[:, c, :n], in0=xt[:, c, :n], scalar=GF[:, c:c + 1], in1=rs[:, :n],
                                                                                                                  op0=ALU.mult, op1=ALU.mult), reads=[bx, brs, b_vec], writes=[bot])
                    P.dma("pool", lambda e, ot=ot, t0=t0, n=n: e.dma_start(out=outT.ap()[:, t0 - C:t0 - C + n].rearrange("(c p) t -> p c t", p=128), in_=ot[:, :, :n]), reads=[bot])
            P.flush(final=(last))

    top.close()
    P.close()
    return nc, P


def _fm(v, n):
    return np.ascontiguousarray(np.asarray(v, np.float32).reshape(n, 128).T)


def prepare_inputs(inputs, S, C, L, n_cores=8):
    f = lambda k: np.asarray(inputs[k], np.float32)
    x, c, ctx, c_ctx = f("x"), f("c"), f("ctx"), f("c_ctx")
    B = x.shape[0]
    w_in = f("w_in")
    lx, lg = w_in[:, :, 0:512], w_in[:, :, 512:1024]
    gq = w_in[:, :, 1024:1536]
    gk = w_in[:, :, 1536:1664]
    gv = w_in[:, :, 1664:1792]
    dq = w_in[:, :, 1792:2304]
    dk = w_in[:, :, 2304:2816]
    dv = w_in[:, :, 2816:3328]
    gkd = np.concatenate([gk[:, :, 0:64], gk[:, :, 0:64], gk[:, :, 64:128], gk[:, :, 64:128]], axis=2)
    w_a = np.ascontiguousarray(np.concatenate([gq, gkd, gv, dq, dk, dv, lx, lg], axis=2))
    w_g = np.ascontiguousarray(w_in[:, :, 3328:6400])

    def bd(w):
        o = np.zeros((L, 2, 4, 128, 128), np.float32)
        for ch in range(4):
            for k in range(2):
                o[:, :, ch, k * 64:(k + 1) * 64, k * 64:(k + 1) * 64] = w[:, :, ch * 2 + k]
        return o

    conv_w, conv_b = f("conv_w"), f("conv_b")
    convp = np.zeros((L, 128, 4, 5), np.float32)
    for l in range(L):
        for j in range(4):
            convp[l, :, :, j] = _fm(conv_w[l, j], 4)
        convp[l, :, :, 4] = _fm(conv_b[l], 4)
    lruv = np.zeros((L, 128, 2, 4, 3), np.float32)
    for l in range(L):
        for d in range(2):
            lruv[l, :, d, :, 0] = _fm(f("b_rg")[l, d], 4)
            lruv[l, :, d, :, 1] = _fm(f("b_ig")[l, d], 4)
            lruv[l, :, d, :, 2] = _fm(f("lru_lambda")[l, d], 4)
    rows = S // GRID_W
    row = np.broadcast_to(np.arange(rows, dtype=np.float32)[:, None], (rows, GRID_W)).reshape(-1)
    colp = np.broadcast_to(np.arange(GRID_W, dtype=np.float32)[None, :], (rows, GRID_W)).reshape(-1)
    inv = (np.float32(10000.0) ** (-np.arange(16, dtype=np.float32) * np.float32(2.0) / np.float32(32))).astype(np.float32)
    ang = np.concatenate([row[:, None] * inv, colp[:, None] * inv], axis=-1).astype(np.float32)
    rope = np.zeros((C + S, 64), np.float32)
    rope[:C, 0:32] = 1.0
    rope[C:, 0:32] = np.cos(ang)
    rope[C:, 32:64] = np.sin(ang)
    shared = {
        "w_mod": f("w_mod"),
        "b_mod": np.stack([_fm(f("b_mod")[l], 48) for l in range(L)]),
        "g1": np.stack([_fm(f("norm1_g")[l], 8) for l in range(L)]),
        "g2": np.stack([_fm(f("norm2_g")[l], 8) for l in range(L)]),
        "gfin": _fm(f("final_g"), 8),
        "w_a": w_a, "w_g": w_g,
        "b_gate": np.stack([_fm(f("b_gate")[l], 24) for l in range(L)]),
        "convp": convp,
        "w_rgbd": bd(f("w_rg")), "w_igbd": bd(f("w_ig")),
        "lruv": lruv,
        "qkg": np.ascontiguousarray(np.stack([f("q_norm_g"), f("k_norm_g")], axis=1)),
        "lamv": np.ascontiguousarray(np.stack([f("lambda_q1"), f("lambda_k1"), f("lambda_q2"), f("lambda_k2")], axis=1)),
        "subg": np.ascontiguousarray(f("subln_g")[:, :, None]),
        "w_br": f("w_branch"), "w_out": f("w_out"), "w_up": f("w_up"), "w_down": f("w_down"),
        "rope": rope,
    }
    owners = core_owners(B, n_cores)
    zeros = {k: np.zeros_like(v) for k, v in shared.items()}
    in_maps = []
    for core in range(n_cores):
        if core in owners:
            b = owners[core]
            m = dict(shared)
            m["xin"] = np.ascontiguousarray(np.concatenate([ctx[b].T, x[b].T], axis=1))
            cv = np.zeros((128, 8, 2), np.float32)
            cv[:, :, 0] = _fm(c[b], 8)
            cv[:, :, 1] = _fm(c_ctx, 8)
            m["cvec"] = cv
        else:
            m = dict(zeros)
            m["xin"] = np.zeros((D_MODEL, C + S), np.float32)
            m["cvec"] = np.zeros((128, 8, 2), np.float32)
        in_maps.append(m)
    return in_maps


def core_owners(B, n_cores):
    if n_cores == 8 and B == 4:
        return {0: 0, 1: 1, 4: 2, 5: 3}
    return {core: core for core in range(min(B, n_cores))}


_CACHE = {}


def run(inputs, n_cores=8):
    x = np.asarray(inputs["x"])
    B, S, _ = x.shape
    C = np.asarray(inputs["ctx"]).shape[1]
    L = np.asarray(inputs["w_mod"]).shape[0]
    key = (S, C, L)
    if key not in _CACHE:
        _CACHE[key] = build_program(S, C, L)[0]
    nc = _CACHE[key]
    in_maps = prepare_inputs(inputs, S, C, L, n_cores)
    res = run_bass_kernel_spmd(nc, in_maps, core_ids=list(range(n_cores)))
    owners = core_owners(B, n_cores)
    by_batch = {b: core for core, b in owners.items()}
    out = np.stack([np.ascontiguousarray(res.results[by_batch[b]]["outT"].T) for b in range(B)], axis=0)
    return out.astype(np.float32)


def kernel(**inputs):
    return run(inputs, n_cores=8)
```
